# Optimizing a Trainium2 kernel written in Bass

```python
import math
import jax, jax.numpy as jnp
from jax import lax
import numpy as np

D_MODEL = 1024
BATCH = 16
SEQ = 2048
DEPTH = 1

D_MIX = D_MODEL
ATT_HEADS = 8
ATT_KV_HEADS = 2
ATT_HEAD_DIM = 64
ATT_WIDTH = ATT_HEADS * ATT_HEAD_DIM
ATT_KV_WIDTH = ATT_KV_HEADS * ATT_HEAD_DIM
WINDOW = 128
BLOCK = 128
REL_BUCKETS = 32
REL_MAX_DIST = 128
ML_HEADS = 4
ML_HEAD_DIM = 128
ML_WIDTH = ML_HEADS * ML_HEAD_DIM
ML_CHUNK = 128
CONV_WIDTH = 3
N_GATE_COLS = 4 * ML_HEADS
D_FF = 4 * D_MODEL
EPS = 1e-6

SPLITS = list(np.cumsum([ATT_WIDTH, ATT_KV_WIDTH, ATT_KV_WIDTH,
                         ML_WIDTH, ML_WIDTH, ML_WIDTH, ML_WIDTH]))
PROJ_WIDTH = ATT_WIDTH + 2 * ATT_KV_WIDTH + 4 * ML_WIDTH + N_GATE_COLS

kernel_name = 'hymba_swa_mlstm_bidir_block'


def rmsnorm(x, g):
    xf = x.astype(jnp.float32)
    y = xf * lax.rsqrt(jnp.mean(xf * xf, axis=-1, keepdims=True) + EPS)
    return (y * g.astype(jnp.float32)).astype(x.dtype)


def t5_bucket(rel):
    nb = REL_BUCKETS // 2
    max_exact = nb // 2
    ret = jnp.where(rel > 0, nb, 0)
    n = jnp.abs(rel)
    nf = jnp.maximum(n, 1).astype(jnp.float32)
    large = max_exact + (jnp.log(nf / max_exact) / math.log(REL_MAX_DIST / max_exact)
                         * (nb - max_exact)).astype(jnp.int32)
    large = jnp.minimum(large, nb - 1)
    return ret + jnp.where(n < max_exact, n, large)


def windowed_sink_attention(q, k, v, rel_bias, sink):
    B, S = q.shape[0], q.shape[1]
    nb = S // BLOCK
    G = ATT_HEADS // ATT_KV_HEADS
    qb = q.reshape(B, nb, BLOCK, ATT_KV_HEADS, G, ATT_HEAD_DIM)
    pad = ((0, 0), (BLOCK, BLOCK), (0, 0), (0, 0))
    kp = jnp.pad(k, pad).reshape(B, nb + 2, BLOCK, ATT_KV_HEADS, ATT_HEAD_DIM)
    vp = jnp.pad(v, pad).reshape(B, nb + 2, BLOCK, ATT_KV_HEADS, ATT_HEAD_DIM)
    kb = jnp.concatenate([kp[:, :-2], kp[:, 1:-1], kp[:, 2:]], axis=2)
    vb = jnp.concatenate([vp[:, :-2], vp[:, 1:-1], vp[:, 2:]], axis=2)
    s = jnp.einsum('bnqhgd,bnkhd->bnhgqk', qb, kb).astype(jnp.float32) * (ATT_HEAD_DIM ** -0.5)
    r = jnp.arange(BLOCK)[:, None]
    c = jnp.arange(3 * BLOCK)[None, :]
    rel = c - BLOCK - r
    bias = rel_bias.astype(jnp.float32)[t5_bucket(rel)]
    bias = bias.transpose(2, 0, 1).reshape(ATT_KV_HEADS, G, BLOCK, 3 * BLOCK)
    kpos = jnp.arange(nb)[:, None] * BLOCK - BLOCK + c
    valid = (jnp.abs(rel) <= WINDOW)[None] & ((kpos >= 0) & (kpos < S))[:, None, :]
    s = jnp.where(valid[None, :, None, None], s + bias[None, None], -jnp.inf)
    sink_l = sink.astype(jnp.float32).reshape(ATT_KV_HEADS, G)[None, None, :, :, None, None]
    m = jnp.maximum(jnp.max(s, axis=-1, keepdims=True), sink_l)
    p = jnp.exp(s - m)
    p = p / (jnp.sum(p, axis=-1, keepdims=True) + jnp.exp(sink_l - m))
    o = jnp.einsum('bnhgqk,bnkhd->bnqhgd', p.astype(v.dtype), vb)
    return o.reshape(B, S, ATT_WIDTH)


def mlstm_scan(q, k, v, log_i, log_f):
    B, H, S, d = q.shape
    nc = S // ML_CHUNK
    L = ML_CHUNK

    def to_chunks(a):
        return jnp.moveaxis(a.reshape((B, H, nc, L) + a.shape[3:]), 2, 0)

    xs = (to_chunks(q), to_chunks(k), to_chunks(v), to_chunks(log_i), to_chunks(log_f))
    lower = jnp.tril(jnp.ones((L, L), dtype=bool))

    def step(carry, inp):
        C, n, m = carry
        qt, kt, vt, li, lf = inp
        b = jnp.cumsum(lf, axis=-1)
        D = jnp.where(lower, b[..., :, None] - b[..., None, :] + li[..., None, :], -jnp.inf)
        inter = b + m[..., None]
        m_t = jnp.maximum(inter, jnp.max(D, axis=-1))
        w_inter = jnp.exp(inter - m_t)
        qk = jnp.einsum('bhtd,bhsd->bhts', qt, kt) * jnp.exp(D - m_t[..., None])
        num = (jnp.einsum('bhts,bhse->bhte', qk, vt)
               + w_inter[..., None] * jnp.einsum('bhtd,bhde->bhte', qt, C))
        den = jnp.sum(qk, axis=-1) + w_inter * jnp.einsum('bhtd,bhd->bht', qt, n)
        h = num / jnp.maximum(jnp.abs(den), jnp.exp(-m_t))[..., None]
        bL = b[..., -1]
        a = bL[..., None] - b + li
        m_new = jnp.maximum(bL + m, jnp.max(a, axis=-1))
        decay = jnp.exp(bL + m - m_new)
        ws = jnp.exp(a - m_new[..., None])
        C = decay[..., None, None] * C + jnp.einsum('bhs,bhsd,bhse->bhde', ws, kt, vt)
        n = decay[..., None] * n + jnp.einsum('bhs,bhsd->bhd', ws, kt)
        return (C, n, m_new), h

    init = (jnp.zeros((B, H, d, d), jnp.float32), jnp.zeros((B, H, d), jnp.float32),
            jnp.zeros((B, H), jnp.float32))
    _, h = lax.scan(step, init, xs)
    return jnp.moveaxis(h, 0, 2).reshape(B, H, S, d)


def centred_depthwise_conv(x, w):
    half = CONV_WIDTH // 2
    return lax.conv_general_dilated(x, w[:, None, :].astype(x.dtype), window_strides=(1,),
                                    padding=[(half, half)],
                                    dimension_numbers=('NWC', 'WIO', 'NWC'),
                                    feature_group_count=x.shape[-1])


def mlstm_mixer(q_in, k_in, v_in, o_pre, gate_pre, gate_bias, conv_w, norm_g):
    B, S = q_in.shape[0], q_in.shape[1]
    qk = jax.nn.silu(centred_depthwise_conv(jnp.concatenate([q_in, k_in], axis=-1), conv_w))

    def heads(a):
        return a.astype(jnp.float32).reshape(B, S, ML_HEADS, ML_HEAD_DIM).transpose(0, 2, 1, 3)

    q = heads(qk[..., :ML_WIDTH])
    k = heads(qk[..., ML_WIDTH:]) * (ML_HEAD_DIM ** -0.5)
    v = heads(v_in)
    g = (gate_pre.astype(jnp.float32) + gate_bias.astype(jnp.float32))
    g = g.reshape(B, S, 4, ML_HEADS).transpose(2, 0, 3, 1)
    i_f, f_f, i_b, f_b = g[0], g[1], g[2], g[3]
    h_f = mlstm_scan(q, k, v, i_f, jax.nn.log_sigmoid(f_f))
    flip = lambda a: jnp.flip(a, axis=2)
    h_b = flip(mlstm_scan(flip(q), flip(k), flip(v), flip(i_b), flip(jax.nn.log_sigmoid(f_b))))
    h = h_f + h_b
    h = h * lax.rsqrt(jnp.mean(h * h, axis=-1, keepdims=True) + EPS)
    h = h * norm_g.astype(jnp.float32).reshape(ML_HEADS, 1, ML_HEAD_DIM)
    h = h.transpose(0, 2, 1, 3).reshape(B, S, ML_WIDTH)
    return (jax.nn.sigmoid(o_pre.astype(jnp.float32)) * h).astype(v_in.dtype)


def setup_inputs(seed: int = 0) -> dict:
    key = jax.random.key(seed)
    ks = jax.random.split(key, 16)
    f32 = jnp.float32
    nrm = lambda k_, shape: jax.random.normal(k_, shape, f32)
    x = nrm(ks[0], (BATCH, SEQ, D_MODEL))
    norm1_g = 1.0 + 0.02 * nrm(ks[1], (DEPTH, D_MODEL))
    w_in = nrm(ks[2], (DEPTH, D_MODEL, PROJ_WIDTH)) * D_MODEL ** -0.5
    i_bias = 0.1 * nrm(ks[3], (DEPTH, 2, ML_HEADS))
    f_bias = jnp.linspace(3.0, 6.0, ML_HEADS, dtype=f32) + 0.1 * nrm(ks[4], (DEPTH, 2, ML_HEADS))
    b_gates = jnp.stack([i_bias[:, 0], f_bias[:, 0], i_bias[:, 1], f_bias[:, 1]],
                        axis=1).reshape(DEPTH, N_GATE_COLS)
    conv_w = nrm(ks[5], (DEPTH, CONV_WIDTH, 2 * ML_WIDTH)) * CONV_WIDTH ** -0.5
    ml_norm_g = 1.0 + 0.02 * nrm(ks[6], (DEPTH, ML_WIDTH))
    sink_logits = nrm(ks[7], (DEPTH, ATT_HEADS))
    w_out = nrm(ks[8], (DEPTH, D_MIX, D_MODEL)) * D_MIX ** -0.5
    norm2_g = 1.0 + 0.02 * nrm(ks[9], (DEPTH, D_MODEL))
    w_up = nrm(ks[10], (DEPTH, D_MODEL, D_FF)) * D_MODEL ** -0.5
    w_down = nrm(ks[11], (DEPTH, D_FF, D_MODEL)) * D_FF ** -0.5
    rel_bias = 0.5 * nrm(ks[12], (REL_BUCKETS, ATT_HEADS))
    final_g = 1.0 + 0.02 * nrm(ks[13], (D_MODEL,))
    return {'x': x, 'norm1_g': norm1_g, 'w_in': w_in, 'b_gates': b_gates, 'conv_w': conv_w,
            'ml_norm_g': ml_norm_g, 'sink_logits': sink_logits, 'w_out': w_out,
            'norm2_g': norm2_g, 'w_up': w_up, 'w_down': w_down, 'rel_bias': rel_bias,
            'final_g': final_g}


def reference(x, norm1_g, w_in, b_gates, conv_w, ml_norm_g, sink_logits, w_out,
              norm2_g, w_up, w_down, rel_bias, final_g):
    B, S = x.shape[0], x.shape[1]
    for l in range(DEPTH):
        u = rmsnorm(x, norm1_g[l])
        proj = u @ w_in[l]
        q_a, k_a, v_a, q_m, k_m, v_m, o_m, g_m = jnp.split(proj, SPLITS, axis=-1)
        att = windowed_sink_attention(
            q_a.reshape(B, S, ATT_HEADS, ATT_HEAD_DIM),
            k_a.reshape(B, S, ATT_KV_HEADS, ATT_HEAD_DIM),
            v_a.reshape(B, S, ATT_KV_HEADS, ATT_HEAD_DIM),
            rel_bias, sink_logits[l])
        ml = mlstm_mixer(q_m, k_m, v_m, o_m, g_m, b_gates[l], conv_w[l], ml_norm_g[l])
        x = x + jnp.concatenate([att, ml], axis=-1) @ w_out[l]
        hid = rmsnorm(x, norm2_g[l]) @ w_up[l]
        x = x + jnp.square(jax.nn.relu(hid)) @ w_down[l]
    return rmsnorm(x, final_g)
```

```python
import math
from contextlib import ExitStack

import numpy as np
import concourse.bass as bass
import concourse.mybir as mybir
from concourse.bass_utils import run_bass_kernel_spmd

F32 = mybir.dt.float32
BF16 = mybir.dt.bfloat16
ALU = mybir.AluOpType
AF = mybir.ActivationFunctionType

NCORES = 8
SEQ = 2048
D = 1024
NT = 16
NSEQ = 2
EPS = 1e-6
NEG = -30000.0
LN_KSCALE = math.log(128.0 ** -0.5)

ENGS = ("pe", "act", "dve", "pool", "sp")


class _Op:
    __slots__ = ("eng", "fn", "deps", "gid", "dma", "done_sem", "done_val", "needed")

    def __init__(self, eng, fn, gid, dma):
        self.eng = eng
        self.fn = fn
        self.deps = []
        self.gid = gid
        self.dma = dma
        self.done_sem = None
        self.done_val = None
        self.needed = False


class Prog:
    N_DMA_SEMS = {"sp": 12, "pool": 8}

    def __init__(self):
        self.ops = []
        self.last_w = {}
        self.readers = {}
        self.dma_hist = {e: [] for e in ENGS}

    REGIONS = ("R1", "R2", "RW", "RA")

    def add(self, eng, fn, r=(), w=(), dma=False, fence=False):
        if not fence:
            r = list(r) + [k for k in w if k in self.REGIONS]
            w = [k for k in w if k not in self.REGIONS]
        isbank = lambda k: isinstance(k, tuple) and k[0] == "bank"
        w = list(w) + [k for k in r if isbank(k)]
        r = [k for k in r if not isbank(k)]
        op = _Op(eng, fn, len(self.ops), dma)
        deps = {}
        for k in r:
            lw = self.last_w.get(k)
            if lw is not None:
                deps[lw.gid] = lw
        for k in w:
            lw = self.last_w.get(k)
            if lw is not None:
                deps[lw.gid] = lw
            for rd in self.readers.get(k, ()):
                deps[rd.gid] = rd
        if dma:
            hist = self.dma_hist[eng]
            n = self.N_DMA_SEMS[eng]
            if len(hist) >= n:
                prev = hist[len(hist) - n]
                deps[prev.gid] = prev
            hist.append(op)
        for d in deps.values():
            if d.eng == "pe" and eng == "pe":
                continue
            op.deps.append(d)
            d.needed = True
        for k in w:
            self.last_w[k] = op
            self.readers[k] = []
        for k in r:
            self.readers.setdefault(k, []).append(op)
        self.ops.append(op)
        return op

    def emit(self, nc, final_wait_ops=()):
        with ExitStack() as es:
            esem = {e: es.enter_context(nc.semaphore("s_" + e)) for e in ENGS}
            dsem = {
                e: [es.enter_context(nc.semaphore("d_%s%d" % (e, i))) for i in range(n)]
                for e, n in self.N_DMA_SEMS.items()
            }
            cnt = {e: 0 for e in ENGS}
            dcnt = {e: 0 for e in ENGS}
            dval = {e: [0] * n for e, n in self.N_DMA_SEMS.items()}
            for op in self.ops:
                if op.dma:
                    i = dcnt[op.eng]
                    dcnt[op.eng] += 1
                    j = i % self.N_DMA_SEMS[op.eng]
                    dval[op.eng][j] += 16
                    op.done_sem = dsem[op.eng][j]
                    op.done_val = dval[op.eng][j]
                elif op.needed:
                    cnt[op.eng] += 1
                    op.done_sem = esem[op.eng]
                    op.done_val = cnt[op.eng]
            per_eng = {e: [o for o in self.ops if o.eng == e] for e in ENGS}
            block = es.enter_context(nc.Block())

            def run(engobj, ename):
                known = {}
                for op in per_eng[ename]:
                    waits = {}
                    for d in op.deps:
                        key = d.done_sem
                        if d.done_val > waits.get(key, (0, None))[0]:
                            waits[key] = (d.done_val, d.done_sem)
                    for key, (v, s) in waits.items():
                        if known.get(key, 0) >= v:
                            continue
                        engobj.wait_ge(s, v)
                        known[key] = v
                    ins = op.fn(engobj)
                    if op.done_sem is not None:
                        ins.then_inc(op.done_sem, 16 if op.dma else 1)
                if ename == "sp":
                    for op in final_wait_ops:
                        if known.get(op.done_sem, 0) < op.done_val:
                            engobj.wait_ge(op.done_sem, op.done_val)
                            known[op.done_sem] = op.done_val

            @block.tensor
            def _(e):
                run(e, "pe")

            @block.scalar
            def _(e):
                run(e, "act")

            @block.vector
            def _(e):
                run(e, "dve")

            @block.gpsimd
            def _(e):
                run(e, "pool")

            @block.sync
            def _(e):
                run(e, "sp")


def _t5_bucket(rel):
    nb, me = 16, 8
    ret = np.where(rel > 0, nb, 0)
    n = np.abs(rel)
    nf = np.maximum(n, 1).astype(np.float32)
    large = me + (np.log(nf / np.float32(me)) / np.float32(math.log(128 / me))
                  * np.float32(nb - me)).astype(np.int32)
    large = np.minimum(large, nb - 1)
    return ret + np.where(n < me, n, large)


def _host_layout(inp):
    f32 = np.float32
    w_in = np.asarray(inp["w_in"], f32)[0]
    blocks = []
    qa = lambda h: list(range(h * 64, (h + 1) * 64))
    for b in range(4):
        blocks.append(qa(b) + qa(4 + b))
    blocks.append(list(range(512, 640)))
    for h in range(4):
        blocks.append(list(range(768 + h * 128, 768 + (h + 1) * 128)))
    for h in range(4):
        blocks.append(list(range(1280 + h * 128, 1280 + (h + 1) * 128)))
    winF = np.stack([w_in[:, c].reshape(8, 128, 128).transpose(1, 0, 2) for c in blocks])
    winT = np.zeros((5, 128, 8, 256), f32)
    for h in range(4):
        c = list(range(1792 + h * 128, 1792 + (h + 1) * 128)) + list(range(2304 + h * 128, 2304 + (h + 1) * 128))
        winT[h] = w_in[:, c].reshape(8, 128, 256).transpose(1, 0, 2)
    c = list(range(640, 768)) + list(range(2816, 2832))
    winT[4, :, :, :144] = w_in[:, c].reshape(8, 128, 144).transpose(1, 0, 2)
    w_out = np.asarray(inp["w_out"], f32)[0].reshape(8, 128, 1024).transpose(1, 0, 2)
    w_up = np.asarray(inp["w_up"], f32)[0].reshape(8, 128, 8, 512).transpose(2, 1, 0, 3)
    w_dn = np.asarray(inp["w_down"], f32)[0].reshape(8, 4, 128, 1024).transpose(0, 2, 1, 3)
    rel_bias = np.asarray(inp["rel_bias"], f32)
    kk = np.arange(128)[:, None]
    qq = np.arange(128)[None, :]
    biasT = np.empty((128, 3, 2, 4, 128), f32)
    for kb in range(3):
        rel = (kb - 1) * 128 + kk - qq
        valid = np.abs(rel) <= 128
        bk = _t5_bucket(rel)
        for h in range(8):
            biasT[:, kb, h // 4, h % 4, :] = np.where(valid, rel_bias[bk, h], f32(NEG))
    consts = np.zeros((128, 4, 128), f32)
    consts[:, 0, :] = np.eye(128, dtype=f32)
    consts[:, 1, :] = (kk <= qq)
    consts[:, 2, :] = (kk >= qq)
    consts[:, 3, :] = 1.0
    convw = np.asarray(inp["conv_w"], f32)[0].reshape(3, 8, 128).transpose(2, 1, 0)
    shared = {
        "winF": np.ascontiguousarray(winF), "winT": winT,
        "w_out": np.ascontiguousarray(w_out), "w_up": np.ascontiguousarray(w_up),
        "w_dn": np.ascontiguousarray(w_dn), "biasT": biasT, "consts": consts,
        "convw": np.ascontiguousarray(convw),
        "g1": np.asarray(inp["norm1_g"], f32).reshape(1, D),
        "g2": np.asarray(inp["norm2_g"], f32).reshape(1, D),
        "gf": np.asarray(inp["final_g"], f32).reshape(1, D),
        "bgate": np.asarray(inp["b_gates"], f32).reshape(1, 16),
        "mlg": np.asarray(inp["ml_norm_g"], f32).reshape(1, 512),
        "sink": np.asarray(inp["sink_logits"], f32).reshape(1, 8),
    }
    return shared


def build_nc(stages=("p1", "attn", "ml", "oproj", "ffn"), taps=(), nseq=NSEQ):
    nc = bass.Bass("TRN2", target_bir_lowering=False)
    dram = lambda name, shape, kind="ExternalInput", dt=F32: nc.dram_tensor(name, list(shape), dt, kind=kind).ap()
    x_d = dram("x", [NSEQ, NT, 128, D])
    winF_d = dram("winF", [13, 128, 8, 128])
    winT_d = dram("winT", [5, 128, 8, 256])
    wout_d = dram("w_out", [128, 8, 1024])
    wup_d = dram("w_up", [8, 128, 8, 512])
    wdn_d = dram("w_dn", [8, 128, 4, 1024])
    biasT_d = dram("biasT", [128, 3, 2, 4, 128])
    consts_d = dram("consts", [128, 4, 128])
    convw_d = dram("convw", [128, 8, 3])
    g1_d = dram("g1", [1, D])
    g2_d = dram("g2", [1, D])
    gf_d = dram("gf", [1, D])
    bgate_d = dram("bgate", [1, 16])
    mlg_d = dram("mlg", [1, 512])
    sink_d = dram("sink", [1, 8])
    out_d = dram("out", [NSEQ, NT, 128, D], kind="ExternalOutput")

    P = Prog()
    final_ops = []
    with ExitStack() as es:
        T = lambda name, shape, dt=F32: es.enter_context(nc.sbuf_tensor("sb_" + name, list(shape), dt))
        consts = T("consts", [128, 4, 128])
        ident = T("ident", [128, 128], BF16)
        bhl = T("bhl", [128, 2, 3072], BF16)
        esink = T("esink", [128, 8])
        bg = T("bg", [128, 16])
        mlg = T("mlg", [128, 512])
        convw = T("convw", [128, 8, 3])
        gbc = T("gbc", [128, D])
        xs = T("xs", [128, 2, D])
        ub = T("ub", [128, D], BF16)
        ssq = T("ssq", [128, NT])
        rstd = T("rstd", [128, NT])
        R1 = T("R1", [128, 8, SEQ], BF16)
        R2 = T("R2", [128, 32768], BF16)
        RW = T("RW", [128, 16384], BF16)
        RA = T("RA", [128, 11264], BF16)
        cst = T("cst", [128, 2050])
        PT = T("PT", [128, 2, 3, 512], BF16)
        tmpS = T("tmpS", [128, 2, 512])
        ctmp = tmpS[:, 0, :]
        gates = T("gates", [128, NT, 16])
        lfa = T("lfa", [128, NT, 2, 4])
        cum = T("cum", [128, NT, 16])
        d1 = T("d1", [128, NT, 8])
        e1 = T("e1", [128, NT, 8])
        e2 = T("e2", [128, NT, 8])
        winv = T("winv", [128, NT, 8])
        dec = T("dec", [128, NT, 8])
        Cst = T("Cst", [128, 2, 2, 132])
        qkw = T("qkw", [128, 2, 128], BF16)
        sm = T("sm", [128, 2, 4])
        smh = T("smh", [128, 16])
        smg = T("smg", [128, 2, 3, 4])
        rec = T("rec", [128, 2, 4])

        x1 = R2[:].bitcast(F32).rearrange("p (t d) -> p t d", d=D)
        QT = R2[:, 0:8192].rearrange("p (b t) -> p b t", t=SEQ)
        KT0 = R2[:, 8192:10240]
        KT1 = R2[:, 10240:12288]
        VA = R2[:, 12288:14368].rearrange("p (t c) -> p t c", c=130)
        mQT = R2[:, 14368:16416]
        mKT = R2[:, 16416:18464]
        mV = R2[:, 18464:20544].rearrange("p (t c) -> p t c", c=130)
        Ktok = R2[:, 20544:22592].rearrange("p (t c) -> p t c", c=128)
        KWf = R2[:, 22592:24640].rearrange("p (t c) -> p t c", c=128)
        KWb = R2[:, 24640:26688].rearrange("p (t c) -> p t c", c=128)
        Call = R2[:, 26688:30912].rearrange("p (a t c) -> p a t c", a=2, c=132)
        mixtok = RW[:].rearrange("p (t c) -> p t c", c=1024)
        wup = [RW[:, i * 4096:(i + 1) * 4096].rearrange("p (k f) -> p k f", f=512) for i in (0, 1)]
        wdn = [RW[:, 8192 + i * 4096:8192 + (i + 1) * 4096].rearrange("p (k f) -> p k f", f=1024) for i in (0, 1)]
        wf = [RA[:, i * 1024:(i + 1) * 1024].rearrange("p (k f) -> p k f", f=128) for i in range(3)]
        wt = [RA[:, 3072 + i * 2048:3072 + (i + 1) * 2048].rearrange("p (k f) -> p k f", f=256) for i in range(2)]
        hb = RA[:, 7168:11264].bitcast(F32).rearrange("p (t c) -> p t c", c=128)
        wo = RA[:, 3072:11264].rearrange("p (k f) -> p k f", f=1024)
        hT = [RA[:, i * 2048:(i + 1) * 2048].rearrange("p (k f) -> p k f", f=512) for i in range(2)]
        hf = xs[:].rearrange("p a (t c) -> p (a t) c", c=128)
        ostg = xs[:].rearrange("p a d -> p (a d)")

        banks = [es.enter_context(nc.psum_tensor("bk%d" % i, [128, 512], F32)) for i in range(8)]
        bkey = lambda i: ("bank", i)
        GBK = ["gbc", ("gbcH", 0), ("gbcH", 1)]
        xsk = lambda slot: [("ostg", slot * 8 + i) for i in range(8)]
        ALLX = [("ostg", i) for i in range(16)]

        tri_f = consts[:, 1, :]
        tri_b = consts[:, 2, :]
        ones = consts[:, 3, :]

        P.add("sp", lambda e: e.dma_start(out=consts[:], in_=consts_d), w=["consts"], dma=True)
        bstage = R2[:, 0:6144].bitcast(F32)
        P.add("sp", lambda e: e.dma_start(out=bstage, in_=biasT_d.rearrange("p a b c d -> p (a b c d)")),
              w=["bstage", "R2"], dma=True)
        P.add("dve", lambda e: e.tensor_scalar(out=bstage, in0=bstage, scalar1=8.0, scalar2=None, op0=ALU.mult),
              r=["R2"], w=["bstage"])
        P.add("dve", lambda e: e.tensor_copy(out=bhl[:, 0, :], in_=bstage), r=["bstage", "R2"], w=["bhi"])
        P.add("dve", lambda e: e.tensor_tensor(out=bstage, in0=bstage, in1=bhl[:, 0, :], op=ALU.subtract),
              r=["bhi", "R2"], w=["bstage"])
        P.add("dve", lambda e: e.tensor_copy(out=bhl[:, 1, :], in_=bstage), r=["bstage", "R2"], w=["blo"])
        P.add("sp", lambda e: e.dma_start(out=esink[:], in_=sink_d.partition_broadcast(128)), w=["esink"], dma=True)
        P.add("sp", lambda e: e.dma_start(out=bg[:], in_=bgate_d.partition_broadcast(128)), w=["bg"], dma=True)
        P.add("sp", lambda e: e.dma_start(out=mlg[:], in_=mlg_d.partition_broadcast(128)), w=["mlg"], dma=True)
        P.add("sp", lambda e: e.dma_start(out=convw[:], in_=convw_d), w=["convw"], dma=True)
        P.add("dve", lambda e: e.tensor_copy(out=ident[:], in_=consts[:, 0, :]), r=["consts"], w=["ident"])
        P.add("act", lambda e: e.activation(out=esink[:], in_=esink[:], func=AF.Exp), r=["esink"], w=["esink"])
        P.add("dve", lambda e: e.memset(cst[:], 0.0), w=["cst"])

        def evac_copy(out_ap, in_ap, r, w):
            P.add("act", lambda e: e.activation(out=out_ap, in_=in_ap, func=AF.Copy), r=r, w=w)

        jbank = tmpS[:, 0, :].bitcast(BF16)
        ub2 = [ub[:], PT[:, 0, 0:2, :].rearrange("p a b -> p (a b)")]
        ub2k = [["ub"], [("PT", 0, 0), ("PT", 0, 1)]]

        def norm_square(src_fn, key_fn, j):
            P.add("act", lambda e: e.activation(out=jbank, in_=src_fn(j), func=AF.Square,
                                                accum_out=ssq[:, j:j + 1]),
                  r=key_fn(j), w=[("tmpS", 0), ("ssq", j)])

        def norm_rstd(lo, hi):
            sk = [("ssq", j) for j in range(lo, hi)]
            rk = [("rstd", j) for j in range(lo, hi)]
            P.add("dve", lambda e: e.tensor_scalar(out=rstd[:, lo:hi], in0=ssq[:, lo:hi], scalar1=1.0 / D,
                                                   scalar2=EPS, op0=ALU.mult, op1=ALU.add), r=sk, w=rk)
            P.add("act", lambda e: e.activation(out=rstd[:, lo:hi], in_=rstd[:, lo:hi], func=AF.Sqrt), r=rk, w=rk)
            P.add("dve", lambda e: e.reciprocal(out=rstd[:, lo:hi], in_=rstd[:, lo:hi]), r=rk, w=rk)

        def norm_stats(src_fn, key_fn):
            for hv in range(2):
                for j in range(hv * 8, hv * 8 + 8):
                    norm_square(src_fn, key_fn, j)
                norm_rstd(hv * 8, hv * 8 + 8)

        def norm_apply(src_fn, key_fn, g_key, dst_key, j):
            u_ap = ub2[j % 2]
            u_k = ub2k[j % 2]
            tb = j % 2
            P.add("dve", lambda e: e.scalar_tensor_tensor(
                out=u_ap, in0=src_fn(j), scalar=rstd[:, j:j + 1], in1=gbc[:], op0=ALU.mult, op1=ALU.mult),
                r=list(key_fn(j)) + [("rstd", j)] + GBK, w=u_k)
            psb = banks[tb][:].bitcast(BF16)
            for kc in range(8):
                P.add("pe", lambda e, kc=kc: e.transpose(
                    out=psb[:, kc * 128:(kc + 1) * 128], in_=u_ap[:, kc * 128:(kc + 1) * 128],
                    identity=ident[:]), r=u_k + ["ident"], w=[bkey(tb)])
            evac_copy(R1[:, :, j * 128:(j + 1) * 128], psb.rearrange("p (a b) -> p a b", b=128),
                      r=[bkey(tb)], w=[(dst_key, j), ("R1t", j), "R1"])

        def norm_T(src_fn, key_fn, g_key, dst_key):
            norm_stats(src_fn, key_fn)
            for j in range(NT):
                norm_apply(src_fn, key_fn, g_key, dst_key, j)

        def fence(key):
            P.add("dve", lambda e: e.memset(smh[:, 15:16], 0.0), w=[key, "smh15"], fence=True)

        def final_store(j, s):
            P.add("dve", lambda e: e.scalar_tensor_tensor(
                out=xs[:, j % 2, :], in0=x1[:, j, :], scalar=rstd[:, j:j + 1], in1=gbc[:],
                op0=ALU.mult, op1=ALU.mult), r=[("x1", j), "R2", ("rstd", j)] + GBK, w=xsk(j % 2))
            final_ops.append(P.add("sp", lambda e: e.dma_start(out=out_d[s, j], in_=xs[:, j % 2, :]),
                                   r=xsk(j % 2), dma=True))

        xpref = [0]
        for s in range(nseq):
            fence("R1")
            if s == 0:
                fence("R2")
            P.add("sp", lambda e: e.dma_start(out=gbc[:], in_=g1_d.partition_broadcast(128)), w=GBK, dma=True)
            for j in range(NT):
                if j < xpref[0]:
                    continue
                P.add("sp", lambda e, j=j, s=s: e.dma_start(out=x1[:, j, :], in_=x_d[s, j]),
                      w=[("x1", j), "R2"], dma=True)
            xpref[0] = 0
            norm_T(lambda j: x1[:, j, :], lambda j: [("x1", j), "R2"], "gbc", "uT")

            fence("R2")
            P.add("dve", lambda e: e.memset(R2[:, 8192:12288], 0.0), w=["KT", "R2"])
            P.add("dve", lambda e: e.memset(VA[:, :, 64:65], 1.0), w=["VAones", "R2"])
            P.add("dve", lambda e: e.memset(VA[:, :, 129:130], 1.0), w=["VAones", "R2"])
            P.add("dve", lambda e: e.memset(mV[:, :, 128:129], 1.0), w=["mVones", "R2"])
            fence("RW")
            fence("RA")

            wf_i = [0]

            def proj_fm(blk, dst_fn, keys_w):
                sl = wf_i[0] % 3
                wf_i[0] += 1
                P.add("pool", lambda e: e.dma_start(out=wf[sl], in_=winF_d[blk]), w=[("wf", sl), "RA"], dma=True)
                for tg in range(4):
                    b = tg % 2
                    for kc in range(8):
                        P.add("pe", lambda e, kc=kc, tg=tg, b=b: e.matmul(
                            banks[b][:], lhsT=wf[sl][:, kc, :], rhs=R1[:, kc, tg * 512:(tg + 1) * 512],
                            start=(kc == 0), stop=(kc == 7)),
                            r=[("wf", sl), "RA"] + [("uT", 4 * tg + i) for i in range(4)] + ["R1"], w=[bkey(b)])
                    dst_fn(tg, b)

            if "attn" in stages:
                for blk in range(4):
                    proj_fm(blk, lambda tg, b, blk=blk: evac_copy(
                        QT[:, blk, tg * 512:(tg + 1) * 512], banks[b][:], r=[bkey(b)], w=[("QT", tg), "R2"]), None)

                def kdst(tg, b):
                    P.add("act", lambda e: e.activation(out=KT0[0:64, tg * 512:(tg + 1) * 512],
                                                        in_=banks[b][0:64, :], func=AF.Copy),
                          r=[bkey(b), "KT"], w=[("KTa", tg), "R2"])
                    P.add("dve", lambda e: e.tensor_copy(out=KT1[64:128, tg * 512:(tg + 1) * 512],
                                                         in_=banks[b][64:128, :]),
                          r=[bkey(b), "KT"], w=[("KTb", tg), "R2"])
                proj_fm(4, kdst, None)

            P.add("pool", lambda e: e.dma_start(out=wt[0], in_=winT_d[4]), w=[("wt", 0), "RA"], dma=True)
            for j in range(NT):
                b = 2 + j % 2
                for kc in range(8):
                    P.add("pe", lambda e, kc=kc, j=j, b=b: e.matmul(
                        banks[b][:, 0:144], lhsT=R1[:, kc, j * 128:(j + 1) * 128], rhs=wt[0][:, kc, 0:144],
                        start=(kc == 0), stop=(kc == 7)),
                        r=[("wt", 0), "RA", ("uT", j), "R1"], w=[bkey(b)])
                P.add("dve", lambda e, j=j, b=b: e.tensor_copy(
                    out=VA[:, j, :].rearrange("p (g c) -> p g c", c=65)[:, :, 0:64],
                    in_=banks[b][:, 0:128].rearrange("p (g c) -> p g c", c=64)),
                    r=[bkey(b), "VAones"], w=[("VA", j), "R2"])
                P.add("dve", lambda e, j=j, b=b: e.tensor_tensor(out=gates[:, j, :], in0=banks[b][:, 128:144],
                                                                in1=bg[:], op=ALU.add),
                      r=[bkey(b), "bg"], w=[("gates", j)])

            def attn_gen():
                for j in range(NT):
                    for g in range(2):
                        KTg = KT0 if g == 0 else KT1
                        pslot = g
                        kbs = [kb for kb in range(3) if 0 <= j - 1 + kb < NT]
                        for kb in kbs:
                            jk = j - 1 + kb
                            b = 5 + kb
                            P.add("pe", lambda e, jk=jk, b=b, KTg=KTg, j=j: e.matmul(
                                banks[b][:], lhsT=KTg[:, jk * 128:(jk + 1) * 128],
                                rhs=QT[:, :, j * 128:(j + 1) * 128], start=True, stop=False),
                                r=[("KTa", jk // 4), ("KTb", jk // 4), "KT", ("QT", j // 4), "R2"], w=[bkey(b)])
                            boff = (kb * 2 + g) * 512
                            for hl in range(2):
                                P.add("pe", lambda e, b=b, hl=hl, boff=boff: e.matmul(
                                    banks[b][:], lhsT=ident[:], rhs=bhl[:, hl, boff:boff + 512],
                                    start=False, stop=(hl == 1)),
                                    r=["ident", "bhi", "blo"], w=[bkey(b)])
                            P.add("act", lambda e, kb=kb, b=b, pslot=pslot: e.activation(
                                out=PT[:, pslot, kb, :], in_=banks[b][:], func=AF.Exp, scale=0.125),
                                r=[bkey(b)], w=[("PT", pslot, kb)])
                        ob = 4
                        for jh in range(4):
                            for i, kb in enumerate(kbs):
                                jk = j - 1 + kb
                                last = (i == len(kbs) - 1)
                                P.add("pe", lambda e, jh=jh, kb=kb, jk=jk, i=i, ob=ob, g=g, pslot=pslot, last=last: e.matmul(
                                    banks[ob][:, jh * 65:(jh + 1) * 65],
                                    lhsT=PT[:, pslot, kb, jh * 128:(jh + 1) * 128],
                                    rhs=VA[:, jk, g * 65:(g + 1) * 65],
                                    start=(i == 0), stop=last),
                                    r=[("PT", pslot, kb), ("VA", jk), "VAones", "R2"], w=[bkey(ob)])
                        O3 = banks[ob][:, 0:260].rearrange("p (h c) -> p h c", c=65)
                        P.add("dve", lambda e, O3=O3, g=g: e.tensor_tensor(
                            out=sm[:, g, :], in0=O3[:, :, 64], in1=esink[:, g * 4:(g + 1) * 4], op=ALU.add),
                            r=[bkey(ob), "esink"], w=[("sm", g)])
                        P.add("dve", lambda e, g=g: e.reciprocal(out=rec[:, g, :], in_=sm[:, g, :]),
                              r=[("sm", g)], w=[("rec", g)])
                        P.add("dve", lambda e, O3=O3, g=g, j=j: e.tensor_tensor(
                            out=mixtok[:, j, g * 256:(g + 1) * 256].rearrange("p (h c) -> p h c", c=64),
                            in0=O3[:, :, 0:64], in1=rec[:, g, :].unsqueeze(2).to_broadcast([128, 4, 64]),
                            op=ALU.mult),
                            r=[bkey(ob), ("rec", g)], w=[("mixtok", j, g), "RW"])
                    yield

            def ml_gen():
                g4 = gates[:].rearrange("p t (a b) -> p t a b", b=4)
                gk = [("gates", j) for j in range(NT)]
                P.add("act", lambda e: e.activation(out=lfa[:], in_=g4[:, :, 1::2, :], func=AF.Exp, scale=-1.0),
                      r=gk, w=["lfa"])
                P.add("dve", lambda e: e.tensor_scalar(out=lfa[:], in0=lfa[:], scalar1=1.0, scalar2=None,
                                                       op0=ALU.add), r=["lfa"], w=["lfa"])
                P.add("act", lambda e: e.activation(out=lfa[:], in_=lfa[:], func=AF.Ln), r=["lfa"], w=["lfa"])
                P.add("dve", lambda e: e.tensor_scalar(out=lfa[:], in0=lfa[:], scalar1=-1.0, scalar2=None,
                                                       op0=ALU.mult), r=["lfa"], w=["lfa"])
                cb = 3
                for j in range(NT):
                    for di, tri in enumerate((tri_f, tri_b)):
                        P.add("pe", lambda e, j=j, di=di, tri=tri: e.matmul(
                            banks[cb][:, j * 16 + di * 4:j * 16 + di * 4 + 4], lhsT=tri, rhs=lfa[:, j, di, :],
                            start=True, stop=True), r=["lfa", "consts"], w=[bkey(cb)])
                        P.add("pe", lambda e, j=j, di=di: e.matmul(
                            banks[cb][:, j * 16 + 8 + di * 4:j * 16 + 8 + di * 4 + 4], lhsT=ones, rhs=lfa[:, j, di, :],
                            start=True, stop=True), r=["lfa", "consts"], w=[bkey(cb)])
                P.add("dve", lambda e: e.tensor_copy(out=cum[:].rearrange("p t c -> p (t c)"), in_=banks[cb][:, 0:256]),
                      r=[bkey(cb)], w=["cum"])
                bcum = cum[:, :, 0:8].rearrange("p t (a b) -> p t a b", b=4)
                tot = cum[:, :, 8:16]
                d1v = d1[:].rearrange("p t (a b) -> p t a b", b=4)
                P.add("dve", lambda e: e.scalar_tensor_tensor(out=d1v, in0=g4[:, :, 0::2, :], scalar=LN_KSCALE,
                                                              in1=bcum, op0=ALU.add, op1=ALU.subtract),
                      r=gk + ["cum"], w=["d1"])
                P.add("act", lambda e: e.activation(out=e1[:], in_=d1[:], func=AF.Exp), r=["d1"], w=["e1"])
                P.add("act", lambda e: e.activation(out=winv[:], in_=cum[:, :, 0:8], func=AF.Exp, scale=-1.0),
                      r=["cum"], w=["winv"])
                P.add("dve", lambda e: e.tensor_tensor(out=d1[:], in0=d1[:], in1=tot, op=ALU.add),
                      r=["d1", "cum", "e1"], w=["d1"])
                P.add("act", lambda e: e.activation(out=e2[:], in_=d1[:], func=AF.Exp), r=["d1"], w=["e2"])
                P.add("act", lambda e: e.activation(out=dec[:], in_=tot, func=AF.Exp), r=["cum"], w=["dec"])
                yield

                for h in range(4):
                    wsl = 1
                    P.add("pool", lambda e, h=h: e.dma_start(out=wt[wsl], in_=winT_d[h]),
                          w=[("wt", wsl), "RA"], dma=True)

                    def vo_tiles(j0, j1):
                        for j in range(j0, j1, 2):
                            b = 2 + (j // 2) % 2
                            for jj in range(2):
                                for kc in range(8):
                                    P.add("pe", lambda e, kc=kc, j=j, jj=jj, b=b: e.matmul(
                                        banks[b][:, jj * 256:(jj + 1) * 256],
                                        lhsT=R1[:, kc, (j + jj) * 128:(j + jj + 1) * 128], rhs=wt[wsl][:, kc, :],
                                        start=(kc == 0), stop=(kc == 7)),
                                        r=[("wt", wsl), "RA", ("uT", j + jj), "R1"], w=[bkey(b)])
                            pv = banks[b][:].rearrange("p (t c) -> p t c", c=256)
                            P.add("act", lambda e, j=j, pv=pv: e.activation(out=mV[:, j:j + 2, 0:128], in_=pv[:, :, 0:128],
                                                                          func=AF.Copy),
                                  r=[bkey(b), "mVones"], w=[("mV", j), ("mV", j + 1), "R2"])
                            P.add("act", lambda e, j=j, pv=pv, h=h: e.activation(
                                out=mixtok[:, j:j + 2, 512 + h * 128:512 + (h + 1) * 128], in_=pv[:, :, 128:256],
                                func=AF.Sigmoid), r=[bkey(b)], w=[("sigo", h, j), ("sigo", h, j + 1), "RW"])

                    for qk, blk, dstT, dkey in ((0, 5 + h, mQT, "mQT"), (1, 9 + h, mKT, "mKT")):
                        cblk = qk * 4 + h

                        def cdst(tg, b):
                            P.add("act", lambda e: e.activation(out=cst[:, 1 + tg * 512:1 + (tg + 1) * 512],
                                                                in_=banks[b][:], func=AF.Copy),
                                  r=[bkey(b)], w=[("cst", tg)])
                        proj_fm(blk, cdst, None)
                        vo_tiles(qk * 8, qk * 8 + 8)
                        for tg in range(4):
                            rk = [("cst", t) for t in range(max(0, tg - 1), min(4, tg + 2))] + ["cst", "convw"]
                            lo = tg * 512
                            ct = (tmpS[:, 0, :], tmpS[:, 1, :], gbc[:, 0:512], gbc[:, 512:1024])[tg]
                            ck = (("tmpS", 0), ("tmpS", 1), ("gbcH", 0), ("gbcH", 1))[tg]
                            P.add("dve", lambda e, lo=lo, cblk=cblk, ct=ct: e.tensor_scalar(
                                out=ct, in0=cst[:, lo:lo + 512], scalar1=convw[:, cblk, 0:1], scalar2=None,
                                op0=ALU.mult), r=rk, w=[ck])
                            P.add("dve", lambda e, lo=lo, cblk=cblk, ct=ct: e.scalar_tensor_tensor(
                                out=ct, in0=cst[:, lo + 1:lo + 513], scalar=convw[:, cblk, 1:2], in1=ct,
                                op0=ALU.mult, op1=ALU.add), r=rk + [ck], w=[ck])
                            P.add("dve", lambda e, lo=lo, cblk=cblk, ct=ct: e.scalar_tensor_tensor(
                                out=ct, in0=cst[:, lo + 2:lo + 514], scalar=convw[:, cblk, 2:3], in1=ct,
                                op0=ALU.mult, op1=ALU.add), r=rk + [ck], w=[ck])
                            P.add("act", lambda e, lo=lo, dstT=dstT, ct=ct: e.activation(
                                out=dstT[:, lo:lo + 512], in_=ct, func=AF.Silu),
                                r=[ck], w=[(dkey, tg), "R2"])
                        yield
                    for half in range(2):
                        tb = half
                        psb = banks[tb][:].bitcast(BF16)
                        for c8 in range(8):
                            c = half * 8 + c8
                            P.add("pe", lambda e, c=c, c8=c8, psb=psb: e.transpose(
                                out=psb[:, c8 * 128:(c8 + 1) * 128], in_=mKT[:, c * 128:(c + 1) * 128],
                                identity=ident[:]),
                                r=[("mKT", c // 4), "ident", "R2"], w=[bkey(tb)])
                        evac_copy(Ktok[:, half * 8:(half + 1) * 8, :], psb.rearrange("p (a b) -> p a b", b=128),
                                  r=[bkey(tb)], w=[("Ktok", half), "R2"])
                    for di, KW in ((0, KWf), (1, KWb)):
                        P.add("dve", lambda e, di=di, KW=KW, h=h: e.tensor_tensor(
                            out=KW, in0=Ktok, in1=e2[:, :, di * 4 + h:di * 4 + h + 1].to_broadcast([128, NT, 128]),
                            op=ALU.mult), r=[("Ktok", 0), ("Ktok", 1), "e2", "R2"], w=[("KW", di), "R2"])
                    P.add("dve", lambda e: e.memset(Cst[:], 0.0), w=[("C", 0, 0), ("C", 0, 1), ("C", 1, 0), ("C", 1, 1)])
                    P.add("dve", lambda e: e.memset(Call[:, 0, 0, :], 0.0), w=[("Cb", 0, 0), "R2"])
                    P.add("dve", lambda e: e.memset(Call[:, 1, NT - 1, :], 0.0), w=[("Cb", 1, NT - 1), "R2"])
                    for grp in range(5):
                        for di in range(2):
                            col = di * 4 + h
                            KW = KWf if di == 0 else KWb
                            bnk = (2 * grp + di) % 4
                            for i in range(3):
                                step = 3 * grp + i
                                c = step if di == 0 else NT - 1 - step
                                P.add("pe", lambda e, c=c, bnk=bnk, KW=KW, i=i: e.matmul(
                                    banks[bnk][:, i * 129:(i + 1) * 129], lhsT=KW[:, c, :], rhs=mV[:, c, 0:129],
                                    start=True, stop=True),
                                    r=[("KW", di), ("mV", c), "mVones", "R2"], w=[bkey(bnk)])
                            for i in range(3):
                                step = 3 * grp + i
                                c = step if di == 0 else NT - 1 - step
                                cn = c + 1 if di == 0 else c - 1
                                sbuf_, dbuf_ = step % 2, (step + 1) % 2
                                P.add("dve", lambda e, c=c, bnk=bnk, col=col, di=di, i=i, sbuf_=sbuf_, dbuf_=dbuf_:
                                      e.scalar_tensor_tensor(
                                          out=Cst[:, di, dbuf_, 0:129], in0=Cst[:, di, sbuf_, 0:129],
                                          scalar=dec[:, c, col:col + 1],
                                          in1=banks[bnk][:, i * 129:(i + 1) * 129], op0=ALU.mult, op1=ALU.add),
                                      r=[bkey(bnk), ("C", di, sbuf_), "dec"], w=[("C", di, dbuf_)])
                                P.add("act", lambda e, di=di, cn=cn, dbuf_=dbuf_: e.activation(
                                    out=Call[:, di, cn, 0:129], in_=Cst[:, di, dbuf_, 0:129], func=AF.Copy),
                                    r=[("C", di, dbuf_)], w=[("Cb", di, cn), "R2"])
                        if grp == 2:
                            yield
                    yield
                    groups = [(0, 3), (3, 3), (6, 3), (9, 3), (12, 3), (15, 1)]
                    pendB = []

                    def stageB(gi, di, c0, n, nbk):
                        col = di * 4 + h
                        hdst = hf if di == 0 else hb
                        N3 = banks[nbk][:, 0:n * 129].rearrange("p (i c) -> p i c", c=129)
                        P.add("dve", lambda e: e.tensor_tensor(
                            out=smg[:, di, 1, 0:n], in0=smg[:, di, 0, 0:n], in1=winv[:, c0:c0 + n, col],
                            op=ALU.max), r=[("smg", di, 0), "winv"], w=[("smg", di, 1)])
                        P.add("dve", lambda e: e.reciprocal(out=smg[:, di, 2, 0:n], in_=smg[:, di, 1, 0:n]),
                              r=[("smg", di, 1)], w=[("smg", di, 2)])
                        P.add("dve", lambda e: e.tensor_tensor(
                            out=hdst[:, c0:c0 + n, :], in0=N3[:, :, 0:128],
                            in1=smg[:, di, 2, 0:n].unsqueeze(2).to_broadcast([128, n, 128]), op=ALU.mult),
                            r=[bkey(nbk), ("smg", di, 2)],
                            w=[("h", di, c0 + i) for i in range(n)]
                            + ([("ostg", c0 + i) for i in range(n)] if di == 0 else ["RA"]))

                    units = [(gi, di, c0, n) for gi, (c0, n) in enumerate(groups) for di in range(2)]

                    def emit_st(u):
                        gi, di, c0, n = units[u]
                        sbk = u % 2
                        for i in range(n):
                            c = c0 + i
                            P.add("pe", lambda e, c=c, sbk=sbk, i=i: e.matmul(
                                banks[sbk][:, i * 128:(i + 1) * 128], lhsT=mKT[:, c * 128:(c + 1) * 128],
                                rhs=mQT[:, c * 128:(c + 1) * 128], start=True, stop=True),
                                r=[("mKT", c // 4), ("mQT", c // 4), "R2"], w=[bkey(sbk)])

                    emit_st(0)
                    for u, (gi, di, c0, n) in enumerate(units):
                        col = di * 4 + h
                        mask = tri_f if di == 0 else tri_b
                        QK = KWf if di == 0 else KWb
                        sbk = u % 2
                        nbk = 2 + u % 2
                        if u + 1 < len(units):
                            emit_st(u + 1)
                        for i in range(n):
                            c = c0 + i
                            first = (gi == 0 and i == 0)
                            P.add("dve", lambda e, c=c, sbk=sbk, col=col, mask=mask, QK=QK, i=i: e.scalar_tensor_tensor(
                                out=QK[:, c, :], in0=banks[sbk][:, i * 128:(i + 1) * 128],
                                scalar=e1[:, c, col:col + 1], in1=mask, op0=ALU.mult, op1=ALU.mult),
                                r=[bkey(sbk), "e1", "consts"] + ([] if first else [("KW", di)]),
                                w=[("qk", di, c), "R2"] + ([("KW", di)] if first else []))
                        if len(pendB) > 1:
                            stageB(*pendB.pop(0))
                        for i in range(n):
                            c = c0 + i
                            P.add("pe", lambda e, c=c, nbk=nbk, QK=QK, i=i: e.matmul(
                                banks[nbk][:, i * 129:(i + 1) * 129], lhsT=QK[:, c, :], rhs=mV[:, c, 0:129],
                                start=True, stop=False),
                                r=[("qk", di, c), ("KW", di), ("mV", c), "mVones", "R2"], w=[bkey(nbk)])
                            P.add("pe", lambda e, c=c, nbk=nbk, di=di, i=i: e.matmul(
                                banks[nbk][:, i * 129:(i + 1) * 129], lhsT=mQT[:, c * 128:(c + 1) * 128],
                                rhs=Call[:, di, c, 0:129], start=False, stop=True),
                                r=[("mQT", c // 4), ("Cb", di, c), "R2"], w=[bkey(nbk)])
                        N3 = banks[nbk][:, 0:n * 129].rearrange("p (i c) -> p i c", c=129)
                        P.add("act", lambda e, N3=N3, di=di, n=n: e.activation(
                            out=smg[:, di, 0, 0:n], in_=N3[:, :, 128], func=AF.Abs),
                            r=[bkey(nbk)], w=[("smg", di, 0)])
                        pendB.append((gi, di, c0, n, nbk))
                        if u % 4 == 3:
                            yield
                    while pendB:
                        stageB(*pendB.pop(0))
                    hkeys = [("h", di, c) for di in range(2) for c in range(NT)]
                    P.add("dve", lambda e: e.tensor_tensor(out=hf, in0=hf, in1=hb, op=ALU.add),
                          r=hkeys + ["RA"], w=["hsum"] + ALLX)
                    for c in range(NT):
                        P.add("act", lambda e, c=c: e.activation(out=qkw[:, 0, :], in_=hf[:, c, :], func=AF.Square,
                                                                 accum_out=ssq[:, c:c + 1]),
                              r=["hsum"], w=[("qkw", 0), ("ssq", c)])
                    sk = [("ssq", c) for c in range(NT)]
                    P.add("dve", lambda e: e.tensor_scalar(out=ssq[:], in0=ssq[:], scalar1=1.0 / 128, scalar2=EPS,
                                                           op0=ALU.mult, op1=ALU.add), r=sk, w=sk)
                    P.add("act", lambda e: e.activation(out=ssq[:], in_=ssq[:], func=AF.Sqrt), r=sk, w=sk)
                    P.add("dve", lambda e: e.reciprocal(out=ssq[:], in_=ssq[:]), r=sk, w=sk)
                    P.add("dve", lambda e: e.tensor_tensor(out=hf, in0=hf,
                                                           in1=ssq[:].unsqueeze(2).to_broadcast([128, NT, 128]),
                                                           op=ALU.mult), r=["hsum"] + sk, w=["hsum"])
                    P.add("dve", lambda e, h=h: e.tensor_tensor(
                        out=hf, in0=hf,
                        in1=mlg[:, h * 128:(h + 1) * 128].unsqueeze(1).to_broadcast([128, NT, 128]),
                        op=ALU.mult), r=["hsum", "mlg"], w=["hsum"])
                    P.add("dve", lambda e, h=h: e.tensor_tensor(
                        out=mixtok[:, :, 512 + h * 128:512 + (h + 1) * 128], in0=hf,
                        in1=mixtok[:, :, 512 + h * 128:512 + (h + 1) * 128], op=ALU.mult),
                        r=["hsum"] + [("sigo", h, j) for j in range(NT)] + ALLX, w=[("mlout", h), "RW"])
                    yield "F"

            gm = ml_gen() if "ml" in stages else None
            ga = attn_gen() if "attn" in stages else None

            def adv(g):
                try:
                    return next(g), g
                except StopIteration:
                    return None, None

            tick = 0
            while gm is not None:
                tag, gm = adv(gm)
                tick += 1
                if ga is not None:
                    n_att = 3 if tag == "F" else (1 if tick % 8 == 0 else 0)
                    for _ in range(n_att):
                        if ga is not None:
                            _, ga = adv(ga)
            while ga is not None:
                _, ga = adv(ga)

            x1src = lambda j: x1[:, j, :]
            x1key = lambda j: [("x1", j), "R2"]
            fuse_norm2 = ("oproj" in stages) and ("ffn" in stages)
            if "oproj" in stages:
                fence("R2")
                fence("RA")
                for j in range(NT):
                    P.add("sp", lambda e, j=j, s=s: e.dma_start(out=x1[:, j, :], in_=x_d[s, j]),
                          w=[("x1", j), "R2"], dma=True)
                P.add("pool", lambda e: e.dma_start(out=wo, in_=wout_d), w=["wo", "RA"], dma=True)
                fence("R1")
                for j in range(NT):
                    tb = j % 2
                    psb = banks[tb][:].bitcast(BF16)
                    rk = [("mixtok", j, 0), ("mixtok", j, 1)] + [("mlout", h) for h in range(4)] + ["RW"]
                    for c in range(8):
                        P.add("pe", lambda e, c=c, j=j, psb=psb: e.transpose(
                            out=psb[:, c * 128:(c + 1) * 128], in_=mixtok[:, j, c * 128:(c + 1) * 128],
                            identity=ident[:]), r=rk + ["ident"], w=[bkey(tb)])
                    evac_copy(R1[:, :, j * 128:(j + 1) * 128], psb.rearrange("p (a b) -> p a b", b=128),
                              r=[bkey(tb)], w=[("mixT", j), ("R1t", j), "R1"])
                if fuse_norm2:
                    fence("RW")
                    P.add("pool", lambda e: e.dma_start(out=wup[0], in_=wup_d[0]), w=[("wup", 0), "RW"], dma=True)
                    P.add("pool", lambda e: e.dma_start(out=wdn[0], in_=wdn_d[0]), w=[("wdn", 0), "RW"], dma=True)
                    P.add("sp", lambda e: e.dma_start(out=gbc[:], in_=g2_d.partition_broadcast(128)),
                          w=GBK, dma=True)
                for j in range(NT):
                    for dh in range(2):
                        b = 2 + (2 * j + dh) % 4
                        for c in range(8):
                            P.add("pe", lambda e, c=c, j=j, dh=dh, b=b: e.matmul(
                                banks[b][:], lhsT=R1[:, c, j * 128:(j + 1) * 128],
                                rhs=wo[:, c, dh * 512:(dh + 1) * 512], start=(c == 0), stop=(c == 7)),
                                r=[("mixT", j), ("R1t", j), "R1", "wo", "RA"], w=[bkey(b)])
                        P.add("dve", lambda e, j=j, dh=dh, b=b: e.tensor_tensor(
                            out=x1[:, j, dh * 512:(dh + 1) * 512], in0=banks[b][:],
                            in1=x1[:, j, dh * 512:(dh + 1) * 512], op=ALU.add),
                            r=[bkey(b), ("x1", j)], w=[("x1", j), "R2"])
                    if fuse_norm2:
                        norm_square(x1src, x1key, j)
                        if j % 8 == 7:
                            norm_rstd(j - 7, j + 1)
                        if j >= 8:
                            norm_apply(x1src, x1key, "gbc", "xnT", j - 8)
                if fuse_norm2:
                    for j in range(8, NT):
                        norm_apply(x1src, x1key, "gbc", "xnT", j)
            else:
                fence("R2")
                for j in range(NT):
                    P.add("sp", lambda e, j=j, s=s: e.dma_start(out=x1[:, j, :], in_=x_d[s, j]),
                          w=[("x1", j), "R2"], dma=True)

            if "ffn" in stages:
                if not fuse_norm2:
                    fence("R1")
                    fence("RW")
                    P.add("sp", lambda e: e.dma_start(out=gbc[:], in_=g2_d.partition_broadcast(128)),
                          w=GBK, dma=True)
                    norm_T(x1src, x1key, "gbc", "xnT")
                fence("RA")
                def ffn_up(fb, tg):
                    sl = fb % 2
                    hs = (fb * 4 + tg) % 2
                    if tg == 0:
                        if not (fuse_norm2 and fb == 0):
                            P.add("pool", lambda e: e.dma_start(out=wup[sl], in_=wup_d[fb]),
                                  w=[("wup", sl), "RW"], dma=True)
                            P.add("pool", lambda e: e.dma_start(out=wdn[sl], in_=wdn_d[fb]),
                                  w=[("wdn", sl), "RW"], dma=True)
                        if fb == 7:
                            P.add("sp", lambda e: e.dma_start(out=gbc[:], in_=gf_d.partition_broadcast(128)),
                                  w=GBK, dma=True)
                    for fc in range(4):
                        b = 4 + fc
                        for kc in range(8):
                            P.add("pe", lambda e, kc=kc, fc=fc, b=b: e.matmul(
                                banks[b][:], lhsT=wup[sl][:, kc, fc * 128:(fc + 1) * 128],
                                rhs=R1[:, kc, tg * 512:(tg + 1) * 512], start=(kc == 0), stop=(kc == 7)),
                                r=[("wup", sl), "RW", "R1"] + [("xnT", 4 * tg + i) for i in range(4)],
                                w=[bkey(b)])
                        ts = fc % 2
                        P.add("act", lambda e, b=b, ts=ts: e.activation(out=tmpS[:, ts, :], in_=banks[b][:],
                                                                        func=AF.Relu),
                              r=[bkey(b)], w=[("tmpS", ts)])
                        P.add("act", lambda e, fc=fc, ts=ts: e.activation(
                            out=hT[hs][:, fc, :], in_=tmpS[:, ts, :], func=AF.Square),
                            r=[("tmpS", ts)], w=[("hT", hs, fc), "RA"])

                def ffn_down(fb, tg, s):
                    sl = fb % 2
                    hs = (fb * 4 + tg) % 2
                    if fb == 7 and tg == 3 and s + 1 < nseq:
                        for j in range(12):
                            P.add("sp", lambda e, j=j: e.dma_start(out=x1[:, j, :], in_=x_d[s + 1, j]),
                                  w=[("x1", j), "R2"], dma=True)
                        xpref[0] = 12
                    for t in range(4):
                        j = tg * 4 + t
                        for dh in range(2):
                            b = 2 * (t % 2) + dh
                            for fc in range(4):
                                P.add("pe", lambda e, fc=fc, t=t, dh=dh, b=b: e.matmul(
                                    banks[b][:], lhsT=hT[hs][:, fc, t * 128:(t + 1) * 128],
                                    rhs=wdn[sl][:, fc, dh * 512:(dh + 1) * 512],
                                    start=(fc == 0), stop=(fc == 3)),
                                    r=[("hT", hs, fc), "RA", ("wdn", sl), "RW"], w=[bkey(b)])
                            P.add("dve", lambda e, j=j, dh=dh, b=b: e.tensor_tensor(
                                out=x1[:, j, dh * 512:(dh + 1) * 512], in0=banks[b][:],
                                in1=x1[:, j, dh * 512:(dh + 1) * 512], op=ALU.add),
                                r=[bkey(b), ("x1", j)], w=[("x1", j), "R2"])
                        if fb == 7:
                            norm_square(x1src, x1key, j)
                    if fb == 7:
                        norm_rstd(tg * 4, tg * 4 + 4)
                        for j in range(tg * 4, tg * 4 + 4):
                            final_store(j, s)

                steps = [(fb, tg) for fb in range(8) for tg in range(4)]
                ffn_up(*steps[0])
                for i, (fb, tg) in enumerate(steps):
                    if i + 1 < len(steps):
                        ffn_up(*steps[i + 1])
                    ffn_down(fb, tg, s)
            if "ffn" not in stages:
                P.add("sp", lambda e: e.dma_start(out=gbc[:], in_=gf_d.partition_broadcast(128)), w=GBK, dma=True)
                norm_stats(x1src, x1key)
                for j in range(NT):
                    final_store(j, s)

        for name, getter, shape, dt in taps:
            t_d = dram(name, shape, kind="ExternalOutput", dt=dt)
            loc = dict(locals())
            ap = getter(loc)
            keys = list(P.last_w.keys())
            final_ops.append(P.add("sp", lambda e, t_d=t_d, ap=ap: e.dma_start(out=t_d, in_=ap), r=keys, dma=True))
        P.emit(nc, final_wait_ops=final_ops)
    return nc


_NC_CACHE = {}


def kernel(**inputs):
    shared = _host_layout(inputs)
    x = np.asarray(inputs["x"], np.float32)
    xs = x.reshape(NCORES, NSEQ, NT, 128, D)
    if "nc" not in _NC_CACHE:
        _NC_CACHE["nc"] = build_nc()
    nc = _NC_CACHE["nc"]
    in_maps = []
    for c in range(NCORES):
        m = dict(shared)
        m["x"] = np.ascontiguousarray(xs[c])
        in_maps.append(m)
    res = run_bass_kernel_spmd(nc, in_maps, core_ids=list(range(NCORES)))
    out = np.stack([np.asarray(r["out"], np.float32) for r in res.results])
    return out.reshape(16, SEQ, D)
```

```python
import math
from contextlib import ExitStack

import numpy as np
import concourse.bass as bass
import concourse.mybir as mybir
from concourse.bass_utils import run_bass_kernel_spmd

F32 = mybir.dt.float32
BF16 = mybir.dt.bfloat16
ALU = mybir.AluOpType
AF = mybir.ActivationFunctionType

NCORES = 8
SEQ = 2048
D = 1024
NT = 16
NSEQ = 2
EPS = 1e-6
NEG = -30000.0
LN_KSCALE = math.log(128.0 ** -0.5)

ENGS = ("pe", "act", "dve", "pool", "sp")


class _Op:
    __slots__ = ("eng", "fn", "deps", "gid", "dma", "done_sem", "done_val", "needed")

    def __init__(self, eng, fn, gid, dma):
        self.eng = eng
        self.fn = fn
        self.deps = []
        self.gid = gid
        self.dma = dma
        self.done_sem = None
        self.done_val = None
        self.needed = False


class Prog:
    N_DMA_SEMS = {"sp": 12, "pool": 8}

    def __init__(self):
        self.ops = []
        self.last_w = {}
        self.readers = {}
        self.dma_hist = {e: [] for e in ENGS}

    REGIONS = ("R1", "R2", "RW", "RA")

    def add(self, eng, fn, r=(), w=(), dma=False, fence=False):
        if not fence:
            r = list(r) + [k for k in w if k in self.REGIONS]
            w = [k for k in w if k not in self.REGIONS]
        isbank = lambda k: isinstance(k, tuple) and k[0] == "bank"
        w = list(w) + [k for k in r if isbank(k)]
        r = [k for k in r if not isbank(k)]
        op = _Op(eng, fn, len(self.ops), dma)
        deps = {}
        for k in r:
            lw = self.last_w.get(k)
            if lw is not None:
                deps[lw.gid] = lw
        for k in w:
            lw = self.last_w.get(k)
            if lw is not None:
                deps[lw.gid] = lw
            for rd in self.readers.get(k, ()):
                deps[rd.gid] = rd
        if dma:
            hist = self.dma_hist[eng]
            n = self.N_DMA_SEMS[eng]
            if len(hist) >= n:
                prev = hist[len(hist) - n]
                deps[prev.gid] = prev
            hist.append(op)
        for d in deps.values():
            if d.eng == "pe" and eng == "pe":
                continue
            op.deps.append(d)
            d.needed = True
        for k in w:
            self.last_w[k] = op
            self.readers[k] = []
        for k in r:
            self.readers.setdefault(k, []).append(op)
        self.ops.append(op)
        return op

    def emit(self, nc, final_wait_ops=()):
        with ExitStack() as es:
            esem = {e: es.enter_context(nc.semaphore("s_" + e)) for e in ENGS}
            dsem = {
                e: [es.enter_context(nc.semaphore("d_%s%d" % (e, i))) for i in range(n)]
                for e, n in self.N_DMA_SEMS.items()
            }
            cnt = {e: 0 for e in ENGS}
            dcnt = {e: 0 for e in ENGS}
            dval = {e: [0] * n for e, n in self.N_DMA_SEMS.items()}
            for op in self.ops:
                if op.dma:
                    i = dcnt[op.eng]
                    dcnt[op.eng] += 1
                    j = i % self.N_DMA_SEMS[op.eng]
                    dval[op.eng][j] += 16
                    op.done_sem = dsem[op.eng][j]
                    op.done_val = dval[op.eng][j]
                elif op.needed:
                    cnt[op.eng] += 1
                    op.done_sem = esem[op.eng]
                    op.done_val = cnt[op.eng]
            per_eng = {e: [o for o in self.ops if o.eng == e] for e in ENGS}
            block = es.enter_context(nc.Block())

            def run(engobj, ename):
                known = {}
                for op in per_eng[ename]:
                    waits = {}
                    for d in op.deps:
                        key = d.done_sem
                        if d.done_val > waits.get(key, (0, None))[0]:
                            waits[key] = (d.done_val, d.done_sem)
                    for key, (v, s) in waits.items():
                        if known.get(key, 0) >= v:
                            continue
                        engobj.wait_ge(s, v)
                        known[key] = v
                    ins = op.fn(engobj)
                    if op.done_sem is not None:
                        ins.then_inc(op.done_sem, 16 if op.dma else 1)
                if ename == "sp":
                    for op in final_wait_ops:
                        if known.get(op.done_sem, 0) < op.done_val:
                            engobj.wait_ge(op.done_sem, op.done_val)
                            known[op.done_sem] = op.done_val

            @block.tensor
            def _(e):
                run(e, "pe")

            @block.scalar
            def _(e):
                run(e, "act")

            @block.vector
            def _(e):
                run(e, "dve")

            @block.gpsimd
            def _(e):
                run(e, "pool")

            @block.sync
            def _(e):
                run(e, "sp")


def _t5_bucket(rel):
    nb, me = 16, 8
    ret = np.where(rel > 0, nb, 0)
    n = np.abs(rel)
    nf = np.maximum(n, 1).astype(np.float32)
    large = me + (np.log(nf / np.float32(me)) / np.float32(math.log(128 / me))
                  * np.float32(nb - me)).astype(np.int32)
    large = np.minimum(large, nb - 1)
    return ret + np.where(n < me, n, large)


def _host_layout(inp):
    f32 = np.float32
    w_in = np.asarray(inp["w_in"], f32)[0]
    blocks = []
    qa = lambda h: list(range(h * 64, (h + 1) * 64))
    for b in range(4):
        blocks.append(qa(b) + qa(4 + b))
    blocks.append(list(range(512, 640)))
    for h in range(4):
        blocks.append(list(range(768 + h * 128, 768 + (h + 1) * 128)))
    for h in range(4):
        blocks.append(list(range(1280 + h * 128, 1280 + (h + 1) * 128)))
    winF = np.stack([w_in[:, c].reshape(8, 128, 128).transpose(1, 0, 2) for c in blocks])
    winT = np.zeros((5, 128, 8, 256), f32)
    for h in range(4):
        c = list(range(1792 + h * 128, 1792 + (h + 1) * 128)) + list(range(2304 + h * 128, 2304 + (h + 1) * 128))
        winT[h] = w_in[:, c].reshape(8, 128, 256).transpose(1, 0, 2)
    c = list(range(640, 768)) + list(range(2816, 2832))
    winT[4, :, :, :144] = w_in[:, c].reshape(8, 128, 144).transpose(1, 0, 2)
    w_out = np.asarray(inp["w_out"], f32)[0].reshape(8, 128, 1024).transpose(1, 0, 2)
    w_up = np.asarray(inp["w_up"], f32)[0].reshape(8, 128, 8, 512).transpose(2, 1, 0, 3)
    w_dn = np.asarray(inp["w_down"], f32)[0].reshape(8, 4, 128, 1024).transpose(0, 2, 1, 3)
    rel_bias = np.asarray(inp["rel_bias"], f32)
    kk = np.arange(128)[:, None]
    qq = np.arange(128)[None, :]
    biasT = np.empty((128, 3, 2, 4, 128), f32)
    for kb in range(3):
        rel = (kb - 1) * 128 + kk - qq
        valid = np.abs(rel) <= 128
        bk = _t5_bucket(rel)
        for h in range(8):
            biasT[:, kb, h // 4, h % 4, :] = np.where(valid, rel_bias[bk, h], f32(NEG))
    consts = np.zeros((128, 4, 128), f32)
    consts[:, 0, :] = np.eye(128, dtype=f32)
    consts[:, 1, :] = (kk <= qq)
    consts[:, 2, :] = (kk >= qq)
    consts[:, 3, :] = 1.0
    convw = np.asarray(inp["conv_w"], f32)[0].reshape(3, 8, 128).transpose(2, 1, 0)
    shared = {
        "winF": np.ascontiguousarray(winF), "winT": winT,
        "w_out": np.ascontiguousarray(w_out), "w_up": np.ascontiguousarray(w_up),
        "w_dn": np.ascontiguousarray(w_dn), "biasT": biasT, "consts": consts,
        "convw": np.ascontiguousarray(convw),
        "g1": np.asarray(inp["norm1_g"], f32).reshape(1, D),
        "g2": np.asarray(inp["norm2_g"], f32).reshape(1, D),
        "gf": np.asarray(inp["final_g"], f32).reshape(1, D),
        "bgate": np.asarray(inp["b_gates"], f32).reshape(1, 16),
        "mlg": np.asarray(inp["ml_norm_g"], f32).reshape(1, 512),
        "sink": np.asarray(inp["sink_logits"], f32).reshape(1, 8),
    }
    return shared


def build_nc(stages=("p1", "attn", "ml", "oproj", "ffn"), taps=(), nseq=NSEQ):
    nc = bass.Bass("TRN2", target_bir_lowering=False)
    dram = lambda name, shape, kind="ExternalInput", dt=F32: nc.dram_tensor(name, list(shape), dt, kind=kind).ap()
    x_d = dram("x", [NSEQ, NT, 128, D])
    winF_d = dram("winF", [13, 128, 8, 128])
    winT_d = dram("winT", [5, 128, 8, 256])
    wout_d = dram("w_out", [128, 8, 1024])
    wup_d = dram("w_up", [8, 128, 8, 512])
    wdn_d = dram("w_dn", [8, 128, 4, 1024])
    biasT_d = dram("biasT", [128, 3, 2, 4, 128])
    consts_d = dram("consts", [128, 4, 128])
    convw_d = dram("convw", [128, 8, 3])
    g1_d = dram("g1", [1, D])
    g2_d = dram("g2", [1, D])
    gf_d = dram("gf", [1, D])
    bgate_d = dram("bgate", [1, 16])
    mlg_d = dram("mlg", [1, 512])
    sink_d = dram("sink", [1, 8])
    out_d = dram("out", [NSEQ, NT, 128, D], kind="ExternalOutput")

    P = Prog()
    final_ops = []
    with ExitStack() as es:
        T = lambda name, shape, dt=F32: es.enter_context(nc.sbuf_tensor("sb_" + name, list(shape), dt))
        consts = T("consts", [128, 4, 128])
        ident = T("ident", [128, 128], BF16)
        bhl = T("bhl", [128, 2, 3072], BF16)
        esink = T("esink", [128, 8])
        bg = T("bg", [128, 16])
        mlg = T("mlg", [128, 512])
        convw = T("convw", [128, 8, 3])
        gbc = T("gbc", [128, D])
        xs = T("xs", [128, 2, D])
        ub = T("ub", [128, D], BF16)
        ssq = T("ssq", [128, NT])
        rstd = T("rstd", [128, NT])
        R1 = T("R1", [128, 8, SEQ], BF16)
        R2 = T("R2", [128, 32768], BF16)
        RW = T("RW", [128, 16384], BF16)
        RA = T("RA", [128, 11264], BF16)
        cst = T("cst", [128, 2050])
        PT = T("PT", [128, 2, 3, 512], BF16)
        tmpS = T("tmpS", [128, 2, 512])
        ctmp = tmpS[:, 0, :]
        gates = T("gates", [128, NT, 16])
        lfa = T("lfa", [128, NT, 2, 4])
        cum = T("cum", [128, NT, 16])
        d1 = T("d1", [128, NT, 8])
        e1 = T("e1", [128, NT, 8])
        e2 = T("e2", [128, NT, 8])
        winv = T("winv", [128, NT, 8])
        dec = T("dec", [128, NT, 8])
        Cst = T("Cst", [128, 2, 2, 132])
        qkw = T("qkw", [128, 2, 128], BF16)
        sm = T("sm", [128, 2, 4])
        smh = T("smh", [128, 16])
        smg = T("smg", [128, 2, 3, 4])
        rec = T("rec", [128, 2, 4])

        x1 = R2[:].bitcast(F32).rearrange("p (t d) -> p t d", d=D)
        QT = R2[:, 0:8192].rearrange("p (b t) -> p b t", t=SEQ)
        KT0 = R2[:, 8192:10240]
        KT1 = R2[:, 10240:12288]
        VA = R2[:, 12288:14368].rearrange("p (t c) -> p t c", c=130)
        mQT = R2[:, 14368:16416]
        mKT = R2[:, 16416:18464]
        mV = R2[:, 18464:20544].rearrange("p (t c) -> p t c", c=130)
        Ktok = R2[:, 20544:22592].rearrange("p (t c) -> p t c", c=128)
        KWf = R2[:, 22592:24640].rearrange("p (t c) -> p t c", c=128)
        KWb = R2[:, 24640:26688].rearrange("p (t c) -> p t c", c=128)
        Call = R2[:, 26688:30912].rearrange("p (a t c) -> p a t c", a=2, c=132)
        mixtok = RW[:].rearrange("p (t c) -> p t c", c=1024)
        wup = [RW[:, i * 4096:(i + 1) * 4096].rearrange("p (k f) -> p k f", f=512) for i in (0, 1)]
        wdn = [RW[:, 8192 + i * 4096:8192 + (i + 1) * 4096].rearrange("p (k f) -> p k f", f=1024) for i in (0, 1)]
        wf = [RA[:, i * 1024:(i + 1) * 1024].rearrange("p (k f) -> p k f", f=128) for i in range(3)]
        wt = [RA[:, 3072 + i * 2048:3072 + (i + 1) * 2048].rearrange("p (k f) -> p k f", f=256) for i in range(2)]
        hb = RA[:, 7168:11264].bitcast(F32).rearrange("p (t c) -> p t c", c=128)
        wo = RA[:, 3072:11264].rearrange("p (k f) -> p k f", f=1024)
        hT = [RA[:, i * 2048:(i + 1) * 2048].rearrange("p (k f) -> p k f", f=512) for i in range(2)]
        hf = xs[:].rearrange("p a (t c) -> p (a t) c", c=128)
        ostg = xs[:].rearrange("p a d -> p (a d)")

        banks = [es.enter_context(nc.psum_tensor("bk%d" % i, [128, 512], F32)) for i in range(8)]
        bkey = lambda i: ("bank", i)
        GBK = ["gbc", ("gbcH", 0), ("gbcH", 1)]
        xsk = lambda slot: [("ostg", slot * 8 + i) for i in range(8)]
        ALLX = [("ostg", i) for i in range(16)]

        tri_f = consts[:, 1, :]
        tri_b = consts[:, 2, :]
        ones = consts[:, 3, :]

        P.add("sp", lambda e: e.dma_start(out=consts[:], in_=consts_d), w=["consts"], dma=True)
        bstage = R2[:, 0:6144].bitcast(F32)
        P.add("sp", lambda e: e.dma_start(out=bstage, in_=biasT_d.rearrange("p a b c d -> p (a b c d)")),
              w=["bstage", "R2"], dma=True)
        P.add("dve", lambda e: e.tensor_scalar(out=bstage, in0=bstage, scalar1=8.0, scalar2=None, op0=ALU.mult),
              r=["R2"], w=["bstage"])
        P.add("dve", lambda e: e.tensor_copy(out=bhl[:, 0, :], in_=bstage), r=["bstage", "R2"], w=["bhi"])
        P.add("dve", lambda e: e.tensor_tensor(out=bstage, in0=bstage, in1=bhl[:, 0, :], op=ALU.subtract),
              r=["bhi", "R2"], w=["bstage"])
        P.add("dve", lambda e: e.tensor_copy(out=bhl[:, 1, :], in_=bstage), r=["bstage", "R2"], w=["blo"])
        P.add("sp", lambda e: e.dma_start(out=esink[:], in_=sink_d.partition_broadcast(128)), w=["esink"], dma=True)
        P.add("sp", lambda e: e.dma_start(out=bg[:], in_=bgate_d.partition_broadcast(128)), w=["bg"], dma=True)
        P.add("sp", lambda e: e.dma_start(out=mlg[:], in_=mlg_d.partition_broadcast(128)), w=["mlg"], dma=True)
        P.add("sp", lambda e: e.dma_start(out=convw[:], in_=convw_d), w=["convw"], dma=True)
        P.add("dve", lambda e: e.tensor_copy(out=ident[:], in_=consts[:, 0, :]), r=["consts"], w=["ident"])
        P.add("act", lambda e: e.activation(out=esink[:], in_=esink[:], func=AF.Exp), r=["esink"], w=["esink"])
        P.add("dve", lambda e: e.memset(cst[:], 0.0), w=["cst"])

        def evac_copy(out_ap, in_ap, r, w):
            P.add("act", lambda e: e.activation(out=out_ap, in_=in_ap, func=AF.Copy), r=r, w=w)

        jbank = tmpS[:, 0, :].bitcast(BF16)
        ub2 = [ub[:], PT[:, 0, 0:2, :].rearrange("p a b -> p (a b)")]
        ub2k = [["ub"], [("PT", 0, 0), ("PT", 0, 1)]]

        def norm_square(src_fn, key_fn, j):
            P.add("act", lambda e: e.activation(out=jbank, in_=src_fn(j), func=AF.Square,
                                                accum_out=ssq[:, j:j + 1]),
                  r=key_fn(j), w=[("tmpS", 0), ("ssq", j)])

        def norm_rstd(lo, hi):
            sk = [("ssq", j) for j in range(lo, hi)]
            rk = [("rstd", j) for j in range(lo, hi)]
            P.add("dve", lambda e: e.tensor_scalar(out=rstd[:, lo:hi], in0=ssq[:, lo:hi], scalar1=1.0 / D,
                                                   scalar2=EPS, op0=ALU.mult, op1=ALU.add), r=sk, w=rk)
            P.add("act", lambda e: e.activation(out=rstd[:, lo:hi], in_=rstd[:, lo:hi], func=AF.Sqrt), r=rk, w=rk)
            P.add("dve", lambda e: e.reciprocal(out=rstd[:, lo:hi], in_=rstd[:, lo:hi]), r=rk, w=rk)

        def norm_stats(src_fn, key_fn):
            for hv in range(2):
                for j in range(hv * 8, hv * 8 + 8):
                    norm_square(src_fn, key_fn, j)
                norm_rstd(hv * 8, hv * 8 + 8)

        def norm_apply(src_fn, key_fn, g_key, dst_key, j):
            u_ap = ub2[j % 2]
            u_k = ub2k[j % 2]
            tb = j % 2
            P.add("dve", lambda e: e.scalar_tensor_tensor(
                out=u_ap, in0=src_fn(j), scalar=rstd[:, j:j + 1], in1=gbc[:], op0=ALU.mult, op1=ALU.mult),
                r=list(key_fn(j)) + [("rstd", j)] + GBK, w=u_k)
            psb = banks[tb][:].bitcast(BF16)
            for kc in range(8):
                P.add("pe", lambda e, kc=kc: e.transpose(
                    out=psb[:, kc * 128:(kc + 1) * 128], in_=u_ap[:, kc * 128:(kc + 1) * 128],
                    identity=ident[:]), r=u_k + ["ident"], w=[bkey(tb)])
            evac_copy(R1[:, :, j * 128:(j + 1) * 128], psb.rearrange("p (a b) -> p a b", b=128),
                      r=[bkey(tb)], w=[(dst_key, j), ("R1t", j), "R1"])

        def norm_T(src_fn, key_fn, g_key, dst_key):
            norm_stats(src_fn, key_fn)
            for j in range(NT):
                norm_apply(src_fn, key_fn, g_key, dst_key, j)

        def fence(key):
            P.add("dve", lambda e: e.memset(smh[:, 15:16], 0.0), w=[key, "smh15"], fence=True)

        def final_store(j, s):
            P.add("dve", lambda e: e.scalar_tensor_tensor(
                out=xs[:, j % 2, :], in0=x1[:, j, :], scalar=rstd[:, j:j + 1], in1=gbc[:],
                op0=ALU.mult, op1=ALU.mult), r=[("x1", j), "R2", ("rstd", j)] + GBK, w=xsk(j % 2))
            final_ops.append(P.add("sp", lambda e: e.dma_start(out=out_d[s, j], in_=xs[:, j % 2, :]),
                                   r=xsk(j % 2), dma=True))

        xpref = [0]
        for s in range(nseq):
            fence("R1")
            if s == 0:
                fence("R2")
            P.add("sp", lambda e: e.dma_start(out=gbc[:], in_=g1_d.partition_broadcast(128)), w=GBK, dma=True)
            for j in range(NT):
                if j < xpref[0]:
                    continue
                P.add("sp", lambda e, j=j, s=s: e.dma_start(out=x1[:, j, :], in_=x_d[s, j]),
                      w=[("x1", j), "R2"], dma=True)
            xpref[0] = 0
            norm_T(lambda j: x1[:, j, :], lambda j: [("x1", j), "R2"], "gbc", "uT")

            fence("R2")
            P.add("dve", lambda e: e.memset(R2[:, 8192:12288], 0.0), w=["KT", "R2"])
            P.add("dve", lambda e: e.memset(VA[:, :, 64:65], 1.0), w=["VAones", "R2"])
            P.add("dve", lambda e: e.memset(VA[:, :, 129:130], 1.0), w=["VAones", "R2"])
            P.add("dve", lambda e: e.memset(mV[:, :, 128:129], 1.0), w=["mVones", "R2"])
            fence("RW")
            fence("RA")

            wf_i = [0]

            def proj_fm(blk, dst_fn, keys_w):
                sl = wf_i[0] % 3
                wf_i[0] += 1
                P.add("pool", lambda e: e.dma_start(out=wf[sl], in_=winF_d[blk]), w=[("wf", sl), "RA"], dma=True)
                for tg in range(4):
                    b = tg % 2
                    for kc in range(8):
                        P.add("pe", lambda e, kc=kc, tg=tg, b=b: e.matmul(
                            banks[b][:], lhsT=wf[sl][:, kc, :], rhs=R1[:, kc, tg * 512:(tg + 1) * 512],
                            start=(kc == 0), stop=(kc == 7)),
                            r=[("wf", sl), "RA"] + [("uT", 4 * tg + i) for i in range(4)] + ["R1"], w=[bkey(b)])
                    dst_fn(tg, b)

            if "attn" in stages:
                for blk in range(4):
                    proj_fm(blk, lambda tg, b, blk=blk: evac_copy(
                        QT[:, blk, tg * 512:(tg + 1) * 512], banks[b][:], r=[bkey(b)], w=[("QT", tg), "R2"]), None)

                def kdst(tg, b):
                    P.add("act", lambda e: e.activation(out=KT0[0:64, tg * 512:(tg + 1) * 512],
                                                        in_=banks[b][0:64, :], func=AF.Copy),
                          r=[bkey(b), "KT"], w=[("KTa", tg), "R2"])
                    P.add("dve", lambda e: e.tensor_copy(out=KT1[64:128, tg * 512:(tg + 1) * 512],
                                                         in_=banks[b][64:128, :]),
                          r=[bkey(b), "KT"], w=[("KTb", tg), "R2"])
                proj_fm(4, kdst, None)

            P.add("pool", lambda e: e.dma_start(out=wt[0], in_=winT_d[4]), w=[("wt", 0), "RA"], dma=True)
            for j in range(NT):
                b = 2 + j % 2
                for kc in range(8):
                    P.add("pe", lambda e, kc=kc, j=j, b=b: e.matmul(
                        banks[b][:, 0:144], lhsT=R1[:, kc, j * 128:(j + 1) * 128], rhs=wt[0][:, kc, 0:144],
                        start=(kc == 0), stop=(kc == 7)),
                        r=[("wt", 0), "RA", ("uT", j), "R1"], w=[bkey(b)])
                P.add("dve", lambda e, j=j, b=b: e.tensor_copy(
                    out=VA[:, j, :].rearrange("p (g c) -> p g c", c=65)[:, :, 0:64],
                    in_=banks[b][:, 0:128].rearrange("p (g c) -> p g c", c=64)),
                    r=[bkey(b), "VAones"], w=[("VA", j), "R2"])
                P.add("dve", lambda e, j=j, b=b: e.tensor_tensor(out=gates[:, j, :], in0=banks[b][:, 128:144],
                                                                in1=bg[:], op=ALU.add),
                      r=[bkey(b), "bg"], w=[("gates", j)])

            def attn_gen():
                for j in range(NT):
                    for g in range(2):
                        KTg = KT0 if g == 0 else KT1
                        pslot = g
                        kbs = [kb for kb in range(3) if 0 <= j - 1 + kb < NT]
                        for kb in kbs:
                            jk = j - 1 + kb
                            b = 5 + kb
                            P.add("pe", lambda e, jk=jk, b=b, KTg=KTg, j=j: e.matmul(
                                banks[b][:], lhsT=KTg[:, jk * 128:(jk + 1) * 128],
                                rhs=QT[:, :, j * 128:(j + 1) * 128], start=True, stop=False),
                                r=[("KTa", jk // 4), ("KTb", jk // 4), "KT", ("QT", j // 4), "R2"], w=[bkey(b)])
                            boff = (kb * 2 + g) * 512
                            for hl in range(2):
                                P.add("pe", lambda e, b=b, hl=hl, boff=boff: e.matmul(
                                    banks[b][:], lhsT=ident[:], rhs=bhl[:, hl, boff:boff + 512],
                                    start=False, stop=(hl == 1)),
                                    r=["ident", "bhi", "blo"], w=[bkey(b)])
                            P.add("act", lambda e, kb=kb, b=b, pslot=pslot: e.activation(
                                out=PT[:, pslot, kb, :], in_=banks[b][:], func=AF.Exp, scale=0.125),
                                r=[bkey(b)], w=[("PT", pslot, kb)])
                        ob = 4
                        for jh in range(4):
                            for i, kb in enumerate(kbs):
                                jk = j - 1 + kb
                                last = (i == len(kbs) - 1)
                                P.add("pe", lambda e, jh=jh, kb=kb, jk=jk, i=i, ob=ob, g=g, pslot=pslot, last=last: e.matmul(
                                    banks[ob][:, jh * 65:(jh + 1) * 65],
                                    lhsT=PT[:, pslot, kb, jh * 128:(jh + 1) * 128],
                                    rhs=VA[:, jk, g * 65:(g + 1) * 65],
                                    start=(i == 0), stop=last),
                                    r=[("PT", pslot, kb), ("VA", jk), "VAones", "R2"], w=[bkey(ob)])
                        O3 = banks[ob][:, 0:260].rearrange("p (h c) -> p h c", c=65)
                        P.add("dve", lambda e, O3=O3, g=g: e.tensor_tensor(
                            out=sm[:, g, :], in0=O3[:, :, 64], in1=esink[:, g * 4:(g + 1) * 4], op=ALU.add),
                            r=[bkey(ob), "esink"], w=[("sm", g)])
                        P.add("dve", lambda e, g=g: e.reciprocal(out=rec[:, g, :], in_=sm[:, g, :]),
                              r=[("sm", g)], w=[("rec", g)])
                        P.add("dve", lambda e, O3=O3, g=g, j=j: e.tensor_tensor(
                            out=mixtok[:, j, g * 256:(g + 1) * 256].rearrange("p (h c) -> p h c", c=64),
                            in0=O3[:, :, 0:64], in1=rec[:, g, :].unsqueeze(2).to_broadcast([128, 4, 64]),
                            op=ALU.mult),
                            r=[bkey(ob), ("rec", g)], w=[("mixtok", j, g), "RW"])
                    yield

            def ml_gen():
                g4 = gates[:].rearrange("p t (a b) -> p t a b", b=4)
                gk = [("gates", j) for j in range(NT)]
                P.add("act", lambda e: e.activation(out=lfa[:], in_=g4[:, :, 1::2, :], func=AF.Exp, scale=-1.0),
                      r=gk, w=["lfa"])
                P.add("dve", lambda e: e.tensor_scalar(out=lfa[:], in0=lfa[:], scalar1=1.0, scalar2=None,
                                                       op0=ALU.add), r=["lfa"], w=["lfa"])
                P.add("act", lambda e: e.activation(out=lfa[:], in_=lfa[:], func=AF.Ln), r=["lfa"], w=["lfa"])
                P.add("dve", lambda e: e.tensor_scalar(out=lfa[:], in0=lfa[:], scalar1=-1.0, scalar2=None,
                                                       op0=ALU.mult), r=["lfa"], w=["lfa"])
                cb = 3
                for j in range(NT):
                    for di, tri in enumerate((tri_f, tri_b)):
                        P.add("pe", lambda e, j=j, di=di, tri=tri: e.matmul(
                            banks[cb][:, j * 16 + di * 4:j * 16 + di * 4 + 4], lhsT=tri, rhs=lfa[:, j, di, :],
                            start=True, stop=True), r=["lfa", "consts"], w=[bkey(cb)])
                        P.add("pe", lambda e, j=j, di=di: e.matmul(
                            banks[cb][:, j * 16 + 8 + di * 4:j * 16 + 8 + di * 4 + 4], lhsT=ones, rhs=lfa[:, j, di, :],
                            start=True, stop=True), r=["lfa", "consts"], w=[bkey(cb)])
                P.add("dve", lambda e: e.tensor_copy(out=cum[:].rearrange("p t c -> p (t c)"), in_=banks[cb][:, 0:256]),
                      r=[bkey(cb)], w=["cum"])
                bcum = cum[:, :, 0:8].rearrange("p t (a b) -> p t a b", b=4)
                tot = cum[:, :, 8:16]
                d1v = d1[:].rearrange("p t (a b) -> p t a b", b=4)
                P.add("dve", lambda e: e.scalar_tensor_tensor(out=d1v, in0=g4[:, :, 0::2, :], scalar=LN_KSCALE,
                                                              in1=bcum, op0=ALU.add, op1=ALU.subtract),
                      r=gk + ["cum"], w=["d1"])
                P.add("act", lambda e: e.activation(out=e1[:], in_=d1[:], func=AF.Exp), r=["d1"], w=["e1"])
                P.add("act", lambda e: e.activation(out=winv[:], in_=cum[:, :, 0:8], func=AF.Exp, scale=-1.0),
                      r=["cum"], w=["winv"])
                P.add("dve", lambda e: e.tensor_tensor(out=d1[:], in0=d1[:], in1=tot, op=ALU.add),
                      r=["d1", "cum", "e1"], w=["d1"])
                P.add("act", lambda e: e.activation(out=e2[:], in_=d1[:], func=AF.Exp), r=["d1"], w=["e2"])
                P.add("act", lambda e: e.activation(out=dec[:], in_=tot, func=AF.Exp), r=["cum"], w=["dec"])
                yield

                for h in range(4):
                    wsl = 1
                    P.add("pool", lambda e, h=h: e.dma_start(out=wt[wsl], in_=winT_d[h]),
                          w=[("wt", wsl), "RA"], dma=True)

                    def vo_tiles(j0, j1):
                        for j in range(j0, j1, 2):
                            b = 2 + (j // 2) % 2
                            for jj in range(2):
                                for kc in range(8):
                                    P.add("pe", lambda e, kc=kc, j=j, jj=jj, b=b: e.matmul(
                                        banks[b][:, jj * 256:(jj + 1) * 256],
                                        lhsT=R1[:, kc, (j + jj) * 128:(j + jj + 1) * 128], rhs=wt[wsl][:, kc, :],
                                        start=(kc == 0), stop=(kc == 7)),
                                        r=[("wt", wsl), "RA", ("uT", j + jj), "R1"], w=[bkey(b)])
                            pv = banks[b][:].rearrange("p (t c) -> p t c", c=256)
                            P.add("act", lambda e, j=j, pv=pv: e.activation(out=mV[:, j:j + 2, 0:128], in_=pv[:, :, 0:128],
                                                                          func=AF.Copy),
                                  r=[bkey(b), "mVones"], w=[("mV", j), ("mV", j + 1), "R2"])
                            P.add("act", lambda e, j=j, pv=pv, h=h: e.activation(
                                out=mixtok[:, j:j + 2, 512 + h * 128:512 + (h + 1) * 128], in_=pv[:, :, 128:256],
                                func=AF.Sigmoid), r=[bkey(b)], w=[("sigo", h, j), ("sigo", h, j + 1), "RW"])

                    for qk, blk, dstT, dkey in ((0, 5 + h, mQT, "mQT"), (1, 9 + h, mKT, "mKT")):
                        cblk = qk * 4 + h

                        def cdst(tg, b):
                            P.add("act", lambda e: e.activation(out=cst[:, 1 + tg * 512:1 + (tg + 1) * 512],
                                                                in_=banks[b][:], func=AF.Copy),
                                  r=[bkey(b)], w=[("cst", tg)])
                        proj_fm(blk, cdst, None)
                        vo_tiles(qk * 8, qk * 8 + 8)
                        for tg in range(4):
                            rk = [("cst", t) for t in range(max(0, tg - 1), min(4, tg + 2))] + ["cst", "convw"]
                            lo = tg * 512
                            ct = (tmpS[:, 0, :], tmpS[:, 1, :], gbc[:, 0:512], gbc[:, 512:1024])[tg]
                            ck = (("tmpS", 0), ("tmpS", 1), ("gbcH", 0), ("gbcH", 1))[tg]
                            P.add("dve", lambda e, lo=lo, cblk=cblk, ct=ct: e.tensor_scalar(
                                out=ct, in0=cst[:, lo:lo + 512], scalar1=convw[:, cblk, 0:1], scalar2=None,
                                op0=ALU.mult), r=rk, w=[ck])
                            P.add("dve", lambda e, lo=lo, cblk=cblk, ct=ct: e.scalar_tensor_tensor(
                                out=ct, in0=cst[:, lo + 1:lo + 513], scalar=convw[:, cblk, 1:2], in1=ct,
                                op0=ALU.mult, op1=ALU.add), r=rk + [ck], w=[ck])
                            P.add("dve", lambda e, lo=lo, cblk=cblk, ct=ct: e.scalar_tensor_tensor(
                                out=ct, in0=cst[:, lo + 2:lo + 514], scalar=convw[:, cblk, 2:3], in1=ct,
                                op0=ALU.mult, op1=ALU.add), r=rk + [ck], w=[ck])
                            P.add("act", lambda e, lo=lo, dstT=dstT, ct=ct: e.activation(
                                out=dstT[:, lo:lo + 512], in_=ct, func=AF.Silu),
                                r=[ck], w=[(dkey, tg), "R2"])
                        yield
                    for half in range(2):
                        tb = half
                        psb = banks[tb][:].bitcast(BF16)
                        for c8 in range(8):
                            c = half * 8 + c8
                            P.add("pe", lambda e, c=c, c8=c8, psb=psb: e.transpose(
                                out=psb[:, c8 * 128:(c8 + 1) * 128], in_=mKT[:, c * 128:(c + 1) * 128],
                                identity=ident[:]),
                                r=[("mKT", c // 4), "ident", "R2"], w=[bkey(tb)])
                        evac_copy(Ktok[:, half * 8:(half + 1) * 8, :], psb.rearrange("p (a b) -> p a b", b=128),
                                  r=[bkey(tb)], w=[("Ktok", half), "R2"])
                    for di, KW in ((0, KWf), (1, KWb)):
                        P.add("dve", lambda e, di=di, KW=KW, h=h: e.tensor_tensor(
                            out=KW, in0=Ktok, in1=e2[:, :, di * 4 + h:di * 4 + h + 1].to_broadcast([128, NT, 128]),
                            op=ALU.mult), r=[("Ktok", 0), ("Ktok", 1), "e2", "R2"], w=[("KW", di), "R2"])
                    P.add("dve", lambda e: e.memset(Cst[:], 0.0), w=[("C", 0, 0), ("C", 0, 1), ("C", 1, 0), ("C", 1, 1)])
                    P.add("dve", lambda e: e.memset(Call[:, 0, 0, :], 0.0), w=[("Cb", 0, 0), "R2"])
                    P.add("dve", lambda e: e.memset(Call[:, 1, NT - 1, :], 0.0), w=[("Cb", 1, NT - 1), "R2"])
                    for grp in range(5):
                        for di in range(2):
                            col = di * 4 + h
                            KW = KWf if di == 0 else KWb
                            bnk = (2 * grp + di) % 4
                            for i in range(3):
                                step = 3 * grp + i
                                c = step if di == 0 else NT - 1 - step
                                P.add("pe", lambda e, c=c, bnk=bnk, KW=KW, i=i: e.matmul(
                                    banks[bnk][:, i * 129:(i + 1) * 129], lhsT=KW[:, c, :], rhs=mV[:, c, 0:129],
                                    start=True, stop=True),
                                    r=[("KW", di), ("mV", c), "mVones", "R2"], w=[bkey(bnk)])
                            for i in range(3):
                                step = 3 * grp + i
                                c = step if di == 0 else NT - 1 - step
                                cn = c + 1 if di == 0 else c - 1
                                sbuf_, dbuf_ = step % 2, (step + 1) % 2
                                P.add("dve", lambda e, c=c, bnk=bnk, col=col, di=di, i=i, sbuf_=sbuf_, dbuf_=dbuf_:
                                      e.scalar_tensor_tensor(
                                          out=Cst[:, di, dbuf_, 0:129], in0=Cst[:, di, sbuf_, 0:129],
                                          scalar=dec[:, c, col:col + 1],
                                          in1=banks[bnk][:, i * 129:(i + 1) * 129], op0=ALU.mult, op1=ALU.add),
                                      r=[bkey(bnk), ("C", di, sbuf_), "dec"], w=[("C", di, dbuf_)])
                                P.add("act", lambda e, di=di, cn=cn, dbuf_=dbuf_: e.activation(
                                    out=Call[:, di, cn, 0:129], in_=Cst[:, di, dbuf_, 0:129], func=AF.Copy),
                                    r=[("C", di, dbuf_)], w=[("Cb", di, cn), "R2"])
                        if grp == 2:
                            yield
                    yield
                    groups = [(0, 3), (3, 3), (6, 3), (9, 3), (12, 3), (15, 1)]
                    pendB = []

                    def stageB(gi, di, c0, n, nbk):
                        col = di * 4 + h
                        hdst = hf if di == 0 else hb
                        N3 = banks[nbk][:, 0:n * 129].rearrange("p (i c) -> p i c", c=129)
                        P.add("dve", lambda e: e.tensor_tensor(
                            out=smg[:, di, 1, 0:n], in0=smg[:, di, 0, 0:n], in1=winv[:, c0:c0 + n, col],
                            op=ALU.max), r=[("smg", di, 0), "winv"], w=[("smg", di, 1)])
                        P.add("dve", lambda e: e.reciprocal(out=smg[:, di, 2, 0:n], in_=smg[:, di, 1, 0:n]),
                              r=[("smg", di, 1)], w=[("smg", di, 2)])
                        P.add("dve", lambda e: e.tensor_tensor(
                            out=hdst[:, c0:c0 + n, :], in0=N3[:, :, 0:128],
                            in1=smg[:, di, 2, 0:n].unsqueeze(2).to_broadcast([128, n, 128]), op=ALU.mult),
                            r=[bkey(nbk), ("smg", di, 2)],
                            w=[("h", di, c0 + i) for i in range(n)]
                            + ([("ostg", c0 + i) for i in range(n)] if di == 0 else ["RA"]))

                    units = [(gi, di, c0, n) for gi, (c0, n) in enumerate(groups) for di in range(2)]

                    def emit_st(u):
                        gi, di, c0, n = units[u]
                        sbk = u % 2
                        for i in range(n):
                            c = c0 + i
                            P.add("pe", lambda e, c=c, sbk=sbk, i=i: e.matmul(
                                banks[sbk][:, i * 128:(i + 1) * 128], lhsT=mKT[:, c * 128:(c + 1) * 128],
                                rhs=mQT[:, c * 128:(c + 1) * 128], start=True, stop=True),
                                r=[("mKT", c // 4), ("mQT", c // 4), "R2"], w=[bkey(sbk)])

                    emit_st(0)
                    for u, (gi, di, c0, n) in enumerate(units):
                        col = di * 4 + h
                        mask = tri_f if di == 0 else tri_b
                        QK = KWf if di == 0 else KWb
                        sbk = u % 2
                        nbk = 2 + u % 2
                        if u + 1 < len(units):
                            emit_st(u + 1)
                        for i in range(n):
                            c = c0 + i
                            first = (gi == 0 and i == 0)
                            P.add("dve", lambda e, c=c, sbk=sbk, col=col, mask=mask, QK=QK, i=i: e.scalar_tensor_tensor(
                                out=QK[:, c, :], in0=banks[sbk][:, i * 128:(i + 1) * 128],
                                scalar=e1[:, c, col:col + 1], in1=mask, op0=ALU.mult, op1=ALU.mult),
                                r=[bkey(sbk), "e1", "consts"] + ([] if first else [("KW", di)]),
                                w=[("qk", di, c), "R2"] + ([("KW", di)] if first else []))
                        if len(pendB) > 1:
                            stageB(*pendB.pop(0))
                        for i in range(n):
                            c = c0 + i
                            P.add("pe", lambda e, c=c, nbk=nbk, QK=QK, i=i: e.matmul(
                                banks[nbk][:, i * 129:(i + 1) * 129], lhsT=QK[:, c, :], rhs=mV[:, c, 0:129],
                                start=True, stop=False),
                                r=[("qk", di, c), ("KW", di), ("mV", c), "mVones", "R2"], w=[bkey(nbk)])
                            P.add("pe", lambda e, c=c, nbk=nbk, di=di, i=i: e.matmul(
                                banks[nbk][:, i * 129:(i + 1) * 129], lhsT=mQT[:, c * 128:(c + 1) * 128],
                                rhs=Call[:, di, c, 0:129], start=False, stop=True),
                                r=[("mQT", c // 4), ("Cb", di, c), "R2"], w=[bkey(nbk)])
                        N3 = banks[nbk][:, 0:n * 129].rearrange("p (i c) -> p i c", c=129)
                        P.add("act", lambda e, N3=N3, di=di, n=n: e.activation(
                            out=smg[:, di, 0, 0:n], in_=N3[:, :, 128], func=AF.Abs),
                            r=[bkey(nbk)], w=[("smg", di, 0)])
                        pendB.append((gi, di, c0, n, nbk))
                        if u % 4 == 3:
                            yield
                    while pendB:
                        stageB(*pendB.pop(0))
                    hkeys = [("h", di, c) for di in range(2) for c in range(NT)]
                    P.add("dve", lambda e: e.tensor_tensor(out=hf, in0=hf, in1=hb, op=ALU.add),
                          r=hkeys + ["RA"], w=["hsum"] + ALLX)
                    for c in range(NT):
                        P.add("act", lambda e, c=c: e.activation(out=qkw[:, 0, :], in_=hf[:, c, :], func=AF.Square,
                                                                 accum_out=ssq[:, c:c + 1]),
                              r=["hsum"], w=[("qkw", 0), ("ssq", c)])
                    sk = [("ssq", c) for c in range(NT)]
                    P.add("dve", lambda e: e.tensor_scalar(out=ssq[:], in0=ssq[:], scalar1=1.0 / 128, scalar2=EPS,
                                                           op0=ALU.mult, op1=ALU.add), r=sk, w=sk)
                    P.add("act", lambda e: e.activation(out=ssq[:], in_=ssq[:], func=AF.Sqrt), r=sk, w=sk)
                    P.add("dve", lambda e: e.reciprocal(out=ssq[:], in_=ssq[:]), r=sk, w=sk)
                    P.add("dve", lambda e: e.tensor_tensor(out=hf, in0=hf,
                                                           in1=ssq[:].unsqueeze(2).to_broadcast([128, NT, 128]),
                                                           op=ALU.mult), r=["hsum"] + sk, w=["hsum"])
                    P.add("dve", lambda e, h=h: e.tensor_tensor(
                        out=hf, in0=hf,
                        in1=mlg[:, h * 128:(h + 1) * 128].unsqueeze(1).to_broadcast([128, NT, 128]),
                        op=ALU.mult), r=["hsum", "mlg"], w=["hsum"])
                    P.add("dve", lambda e, h=h: e.tensor_tensor(
                        out=mixtok[:, :, 512 + h * 128:512 + (h + 1) * 128], in0=hf,
                        in1=mixtok[:, :, 512 + h * 128:512 + (h + 1) * 128], op=ALU.mult),
                        r=["hsum"] + [("sigo", h, j) for j in range(NT)] + ALLX, w=[("mlout", h), "RW"])
                    yield "F"

            gm = ml_gen() if "ml" in stages else None
            ga = attn_gen() if "attn" in stages else None

            def adv(g):
                try:
                    return next(g), g
                except StopIteration:
                    return None, None

            tick = 0
            while gm is not None:
                tag, gm = adv(gm)
                tick += 1
                if ga is not None:
                    n_att = 4 if tag == "F" else 0
                    for _ in range(n_att):
                        if ga is not None:
                            _, ga = adv(ga)
            while ga is not None:
                _, ga = adv(ga)

            x1src = lambda j: x1[:, j, :]
            x1key = lambda j: [("x1", j), "R2"]
            fuse_norm2 = ("oproj" in stages) and ("ffn" in stages)
            if "oproj" in stages:
                fence("R2")
                fence("RA")
                for j in range(NT):
                    P.add("sp", lambda e, j=j, s=s: e.dma_start(out=x1[:, j, :], in_=x_d[s, j]),
                          w=[("x1", j), "R2"], dma=True)
                P.add("pool", lambda e: e.dma_start(out=wo, in_=wout_d), w=["wo", "RA"], dma=True)
                fence("R1")
                for j in range(NT):
                    tb = j % 2
                    psb = banks[tb][:].bitcast(BF16)
                    rk = [("mixtok", j, 0), ("mixtok", j, 1)] + [("mlout", h) for h in range(4)] + ["RW"]
                    for c in range(8):
                        P.add("pe", lambda e, c=c, j=j, psb=psb: e.transpose(
                            out=psb[:, c * 128:(c + 1) * 128], in_=mixtok[:, j, c * 128:(c + 1) * 128],
                            identity=ident[:]), r=rk + ["ident"], w=[bkey(tb)])
                    evac_copy(R1[:, :, j * 128:(j + 1) * 128], psb.rearrange("p (a b) -> p a b", b=128),
                              r=[bkey(tb)], w=[("mixT", j), ("R1t", j), "R1"])
                if fuse_norm2:
                    fence("RW")
                    P.add("pool", lambda e: e.dma_start(out=wup[0], in_=wup_d[0]), w=[("wup", 0), "RW"], dma=True)
                    P.add("pool", lambda e: e.dma_start(out=wdn[0], in_=wdn_d[0]), w=[("wdn", 0), "RW"], dma=True)
                    P.add("sp", lambda e: e.dma_start(out=gbc[:], in_=g2_d.partition_broadcast(128)),
                          w=GBK, dma=True)
                for j in range(NT):
                    for dh in range(2):
                        b = 2 + (2 * j + dh) % 4
                        for c in range(8):
                            P.add("pe", lambda e, c=c, j=j, dh=dh, b=b: e.matmul(
                                banks[b][:], lhsT=R1[:, c, j * 128:(j + 1) * 128],
                                rhs=wo[:, c, dh * 512:(dh + 1) * 512], start=(c == 0), stop=(c == 7)),
                                r=[("mixT", j), ("R1t", j), "R1", "wo", "RA"], w=[bkey(b)])
                        P.add("dve", lambda e, j=j, dh=dh, b=b: e.tensor_tensor(
                            out=x1[:, j, dh * 512:(dh + 1) * 512], in0=banks[b][:],
                            in1=x1[:, j, dh * 512:(dh + 1) * 512], op=ALU.add),
                            r=[bkey(b), ("x1", j)], w=[("x1", j), "R2"])
                    if fuse_norm2:
                        norm_square(x1src, x1key, j)
                        if j % 8 == 7:
                            norm_rstd(j - 7, j + 1)
                        if j >= 8:
                            norm_apply(x1src, x1key, "gbc", "xnT", j - 8)
                if fuse_norm2:
                    for j in range(8, NT):
                        norm_apply(x1src, x1key, "gbc", "xnT", j)
            else:
                fence("R2")
                for j in range(NT):
                    P.add("sp", lambda e, j=j, s=s: e.dma_start(out=x1[:, j, :], in_=x_d[s, j]),
                          w=[("x1", j), "R2"], dma=True)

            if "ffn" in stages:
                if not fuse_norm2:
                    fence("R1")
                    fence("RW")
                    P.add("sp", lambda e: e.dma_start(out=gbc[:], in_=g2_d.partition_broadcast(128)),
                          w=GBK, dma=True)
                    norm_T(x1src, x1key, "gbc", "xnT")
                fence("RA")
                def ffn_up(fb, tg):
                    sl = fb % 2
                    hs = (fb * 4 + tg) % 2
                    if tg == 0:
                        if not (fuse_norm2 and fb == 0):
                            P.add("pool", lambda e: e.dma_start(out=wup[sl], in_=wup_d[fb]),
                                  w=[("wup", sl), "RW"], dma=True)
                            P.add("pool", lambda e: e.dma_start(out=wdn[sl], in_=wdn_d[fb]),
                                  w=[("wdn", sl), "RW"], dma=True)
                        if fb == 7:
                            P.add("sp", lambda e: e.dma_start(out=gbc[:], in_=gf_d.partition_broadcast(128)),
                                  w=GBK, dma=True)
                    for fc in range(4):
                        b = 4 + fc
                        for kc in range(8):
                            P.add("pe", lambda e, kc=kc, fc=fc, b=b: e.matmul(
                                banks[b][:], lhsT=wup[sl][:, kc, fc * 128:(fc + 1) * 128],
                                rhs=R1[:, kc, tg * 512:(tg + 1) * 512], start=(kc == 0), stop=(kc == 7)),
                                r=[("wup", sl), "RW", "R1"] + [("xnT", 4 * tg + i) for i in range(4)],
                                w=[bkey(b)])
                        ts = fc % 2
                        P.add("act", lambda e, b=b, ts=ts: e.activation(out=tmpS[:, ts, :], in_=banks[b][:],
                                                                        func=AF.Relu),
                              r=[bkey(b)], w=[("tmpS", ts)])
                        P.add("act", lambda e, fc=fc, ts=ts: e.activation(
                            out=hT[hs][:, fc, :], in_=tmpS[:, ts, :], func=AF.Square),
                            r=[("tmpS", ts)], w=[("hT", hs, fc), "RA"])

                def ffn_down(fb, tg, s):
                    sl = fb % 2
                    hs = (fb * 4 + tg) % 2
                    if fb == 7 and tg == 3 and s + 1 < nseq:
                        for j in range(12):
                            P.add("sp", lambda e, j=j: e.dma_start(out=x1[:, j, :], in_=x_d[s + 1, j]),
                                  w=[("x1", j), "R2"], dma=True)
                        xpref[0] = 12
                    for t in range(4):
                        j = tg * 4 + t
                        for dh in range(2):
                            b = 2 * (t % 2) + dh
                            for fc in range(4):
                                P.add("pe", lambda e, fc=fc, t=t, dh=dh, b=b: e.matmul(
                                    banks[b][:], lhsT=hT[hs][:, fc, t * 128:(t + 1) * 128],
                                    rhs=wdn[sl][:, fc, dh * 512:(dh + 1) * 512],
                                    start=(fc == 0), stop=(fc == 3)),
                                    r=[("hT", hs, fc), "RA", ("wdn", sl), "RW"], w=[bkey(b)])
                            P.add("dve", lambda e, j=j, dh=dh, b=b: e.tensor_tensor(
                                out=x1[:, j, dh * 512:(dh + 1) * 512], in0=banks[b][:],
                                in1=x1[:, j, dh * 512:(dh + 1) * 512], op=ALU.add),
                                r=[bkey(b), ("x1", j)], w=[("x1", j), "R2"])
                        if fb == 7:
                            norm_square(x1src, x1key, j)
                    if fb == 7:
                        norm_rstd(tg * 4, tg * 4 + 4)
                        for j in range(tg * 4, tg * 4 + 4):
                            final_store(j, s)

                steps = [(fb, tg) for fb in range(8) for tg in range(4)]
                ffn_up(*steps[0])
                for i, (fb, tg) in enumerate(steps):
                    if i + 1 < len(steps):
                        ffn_up(*steps[i + 1])
                    ffn_down(fb, tg, s)
            if "ffn" not in stages:
                P.add("sp", lambda e: e.dma_start(out=gbc[:], in_=gf_d.partition_broadcast(128)), w=GBK, dma=True)
                norm_stats(x1src, x1key)
                for j in range(NT):
                    final_store(j, s)

        for name, getter, shape, dt in taps:
            t_d = dram(name, shape, kind="ExternalOutput", dt=dt)
            loc = dict(locals())
            ap = getter(loc)
            keys = list(P.last_w.keys())
            final_ops.append(P.add("sp", lambda e, t_d=t_d, ap=ap: e.dma_start(out=t_d, in_=ap), r=keys, dma=True))
        P.emit(nc, final_wait_ops=final_ops)
    return nc


_NC_CACHE = {}


def kernel(**inputs):
    shared = _host_layout(inputs)
    x = np.asarray(inputs["x"], np.float32)
    xs = x.reshape(NCORES, NSEQ, NT, 128, D)
    if "nc" not in _NC_CACHE:
        _NC_CACHE["nc"] = build_nc()
    nc = _NC_CACHE["nc"]
    in_maps = []
    for c in range(NCORES):
        m = dict(shared)
        m["x"] = np.ascontiguousarray(xs[c])
        in_maps.append(m)
    res = run_bass_kernel_spmd(nc, in_maps, core_ids=list(range(NCORES)))
    out = np.stack([np.asarray(r["out"], np.float32) for r in res.results])
    return out.reshape(16, SEQ, D)
```

```python
import math
from contextlib import ExitStack

import numpy as np
import concourse.bass as bass
import concourse.mybir as mybir
from concourse.bass_utils import run_bass_kernel_spmd

F32 = mybir.dt.float32
BF16 = mybir.dt.bfloat16
ALU = mybir.AluOpType
AF = mybir.ActivationFunctionType

NCORES = 8
SEQ = 2048
D = 1024
NT = 16
NSEQ = 2
EPS = 1e-6
NEG = -30000.0
LN_KSCALE = math.log(128.0 ** -0.5)

ENGS = ("pe", "act", "dve", "pool", "sp")


class _Op:
    __slots__ = ("eng", "fn", "deps", "gid", "dma", "done_sem", "done_val", "needed")

    def __init__(self, eng, fn, gid, dma):
        self.eng = eng
        self.fn = fn
        self.deps = []
        self.gid = gid
        self.dma = dma
        self.done_sem = None
        self.done_val = None
        self.needed = False


class Prog:
    N_DMA_SEMS = {"sp": 12, "pool": 8}

    def __init__(self):
        self.ops = []
        self.last_w = {}
        self.readers = {}
        self.dma_hist = {e: [] for e in ENGS}

    REGIONS = ("R1", "R2", "RW", "RA")

    def add(self, eng, fn, r=(), w=(), dma=False, fence=False):
        if not fence:
            r = list(r) + [k for k in w if k in self.REGIONS]
            w = [k for k in w if k not in self.REGIONS]
        isbank = lambda k: isinstance(k, tuple) and k[0] == "bank"
        w = list(w) + [k for k in r if isbank(k)]
        r = [k for k in r if not isbank(k)]
        op = _Op(eng, fn, len(self.ops), dma)
        deps = {}
        for k in r:
            lw = self.last_w.get(k)
            if lw is not None:
                deps[lw.gid] = lw
        for k in w:
            lw = self.last_w.get(k)
            if lw is not None:
                deps[lw.gid] = lw
            for rd in self.readers.get(k, ()):
                deps[rd.gid] = rd
        if dma:
            hist = self.dma_hist[eng]
            n = self.N_DMA_SEMS[eng]
            if len(hist) >= n:
                prev = hist[len(hist) - n]
                deps[prev.gid] = prev
            hist.append(op)
        for d in deps.values():
            if d.eng == "pe" and eng == "pe":
                continue
            op.deps.append(d)
            d.needed = True
        for k in w:
            self.last_w[k] = op
            self.readers[k] = []
        for k in r:
            self.readers.setdefault(k, []).append(op)
        self.ops.append(op)
        return op

    def emit(self, nc, final_wait_ops=()):
        with ExitStack() as es:
            esem = {e: es.enter_context(nc.semaphore("s_" + e)) for e in ENGS}
            dsem = {
                e: [es.enter_context(nc.semaphore("d_%s%d" % (e, i))) for i in range(n)]
                for e, n in self.N_DMA_SEMS.items()
            }
            cnt = {e: 0 for e in ENGS}
            dcnt = {e: 0 for e in ENGS}
            dval = {e: [0] * n for e, n in self.N_DMA_SEMS.items()}
            for op in self.ops:
                if op.dma:
                    i = dcnt[op.eng]
                    dcnt[op.eng] += 1
                    j = i % self.N_DMA_SEMS[op.eng]
                    dval[op.eng][j] += 16
                    op.done_sem = dsem[op.eng][j]
                    op.done_val = dval[op.eng][j]
                elif op.needed:
                    cnt[op.eng] += 1
                    op.done_sem = esem[op.eng]
                    op.done_val = cnt[op.eng]
            per_eng = {e: [o for o in self.ops if o.eng == e] for e in ENGS}
            block = es.enter_context(nc.Block())

            def run(engobj, ename):
                known = {}
                for op in per_eng[ename]:
                    waits = {}
                    for d in op.deps:
                        key = d.done_sem
                        if d.done_val > waits.get(key, (0, None))[0]:
                            waits[key] = (d.done_val, d.done_sem)
                    for key, (v, s) in waits.items():
                        if known.get(key, 0) >= v:
                            continue
                        engobj.wait_ge(s, v)
                        known[key] = v
                    ins = op.fn(engobj)
                    if op.done_sem is not None:
                        ins.then_inc(op.done_sem, 16 if op.dma else 1)
                if ename == "sp":
                    for op in final_wait_ops:
                        if known.get(op.done_sem, 0) < op.done_val:
                            engobj.wait_ge(op.done_sem, op.done_val)
                            known[op.done_sem] = op.done_val

            @block.tensor
            def _(e):
                run(e, "pe")

            @block.scalar
            def _(e):
                run(e, "act")

            @block.vector
            def _(e):
                run(e, "dve")

            @block.gpsimd
            def _(e):
                run(e, "pool")

            @block.sync
            def _(e):
                run(e, "sp")


def _t5_bucket(rel):
    nb, me = 16, 8
    ret = np.where(rel > 0, nb, 0)
    n = np.abs(rel)
    nf = np.maximum(n, 1).astype(np.float32)
    large = me + (np.log(nf / np.float32(me)) / np.float32(math.log(128 / me))
                  * np.float32(nb - me)).astype(np.int32)
    large = np.minimum(large, nb - 1)
    return ret + np.where(n < me, n, large)


def _host_layout(inp):
    f32 = np.float32
    w_in = np.asarray(inp["w_in"], f32)[0]
    blocks = []
    qa = lambda h: list(range(h * 64, (h + 1) * 64))
    for b in range(4):
        blocks.append(qa(b) + qa(4 + b))
    blocks.append(list(range(512, 640)))
    for h in range(4):
        blocks.append(list(range(768 + h * 128, 768 + (h + 1) * 128)))
    for h in range(4):
        blocks.append(list(range(1280 + h * 128, 1280 + (h + 1) * 128)))
    winF = np.stack([w_in[:, c].reshape(8, 128, 128).transpose(1, 0, 2) for c in blocks])
    winT = np.zeros((5, 128, 8, 256), f32)
    for h in range(4):
        c = list(range(1792 + h * 128, 1792 + (h + 1) * 128)) + list(range(2304 + h * 128, 2304 + (h + 1) * 128))
        winT[h] = w_in[:, c].reshape(8, 128, 256).transpose(1, 0, 2)
    c = list(range(640, 768)) + list(range(2816, 2832))
    winT[4, :, :, :144] = w_in[:, c].reshape(8, 128, 144).transpose(1, 0, 2)
    w_out = np.asarray(inp["w_out"], f32)[0].reshape(8, 128, 1024).transpose(1, 0, 2)
    w_up = np.asarray(inp["w_up"], f32)[0].reshape(8, 128, 8, 512).transpose(2, 1, 0, 3)
    w_dn = np.asarray(inp["w_down"], f32)[0].reshape(8, 4, 128, 1024).transpose(0, 2, 1, 3)
    rel_bias = np.asarray(inp["rel_bias"], f32)
    kk = np.arange(128)[:, None]
    qq = np.arange(128)[None, :]
    biasT = np.empty((128, 3, 2, 4, 128), f32)
    for kb in range(3):
        rel = (kb - 1) * 128 + kk - qq
        valid = np.abs(rel) <= 128
        bk = _t5_bucket(rel)
        for h in range(8):
            biasT[:, kb, h // 4, h % 4, :] = np.where(valid, rel_bias[bk, h], f32(NEG))
    consts = np.zeros((128, 4, 128), f32)
    consts[:, 0, :] = np.eye(128, dtype=f32)
    consts[:, 1, :] = (kk <= qq)
    consts[:, 2, :] = (kk >= qq)
    consts[:, 3, :] = 1.0
    convw = np.asarray(inp["conv_w"], f32)[0].reshape(3, 8, 128).transpose(2, 1, 0)
    shared = {
        "winF": np.ascontiguousarray(winF), "winT": winT,
        "w_out": np.ascontiguousarray(w_out), "w_up": np.ascontiguousarray(w_up),
        "w_dn": np.ascontiguousarray(w_dn), "biasT": biasT, "consts": consts,
        "convw": np.ascontiguousarray(convw),
        "g1": np.asarray(inp["norm1_g"], f32).reshape(1, D),
        "g2": np.asarray(inp["norm2_g"], f32).reshape(1, D),
        "gf": np.asarray(inp["final_g"], f32).reshape(1, D),
        "bgate": np.asarray(inp["b_gates"], f32).reshape(1, 16),
        "mlg": np.asarray(inp["ml_norm_g"], f32).reshape(1, 512),
        "sink": np.asarray(inp["sink_logits"], f32).reshape(1, 8),
    }
    return shared


def build_nc(stages=("p1", "attn", "ml", "oproj", "ffn"), taps=(), nseq=NSEQ):
    nc = bass.Bass("TRN2", target_bir_lowering=False)
    dram = lambda name, shape, kind="ExternalInput", dt=F32: nc.dram_tensor(name, list(shape), dt, kind=kind).ap()
    x_d = dram("x", [NSEQ, NT, 128, D])
    winF_d = dram("winF", [13, 128, 8, 128])
    winT_d = dram("winT", [5, 128, 8, 256])
    wout_d = dram("w_out", [128, 8, 1024])
    wup_d = dram("w_up", [8, 128, 8, 512])
    wdn_d = dram("w_dn", [8, 128, 4, 1024])
    biasT_d = dram("biasT", [128, 3, 2, 4, 128])
    consts_d = dram("consts", [128, 4, 128])
    convw_d = dram("convw", [128, 8, 3])
    g1_d = dram("g1", [1, D])
    g2_d = dram("g2", [1, D])
    gf_d = dram("gf", [1, D])
    bgate_d = dram("bgate", [1, 16])
    mlg_d = dram("mlg", [1, 512])
    sink_d = dram("sink", [1, 8])
    out_d = dram("out", [NSEQ, NT, 128, D], kind="ExternalOutput")

    P = Prog()
    final_ops = []
    with ExitStack() as es:
        T = lambda name, shape, dt=F32: es.enter_context(nc.sbuf_tensor("sb_" + name, list(shape), dt))
        consts = T("consts", [128, 4, 128])
        ident = T("ident", [128, 128], BF16)
        bhl = T("bhl", [128, 2, 3072], BF16)
        esink = T("esink", [128, 8])
        bg = T("bg", [128, 16])
        mlg = T("mlg", [128, 512])
        convw = T("convw", [128, 8, 3])
        gbc = T("gbc", [128, D])
        xs = T("xs", [128, 2, D])
        ub = T("ub", [128, D], BF16)
        ssq = T("ssq", [128, NT])
        rstd = T("rstd", [128, NT])
        R1 = T("R1", [128, 8, SEQ], BF16)
        R2 = T("R2", [128, 32768], BF16)
        RW = T("RW", [128, 16384], BF16)
        RA = T("RA", [128, 11264], BF16)
        cst = T("cst", [128, 2050])
        PT = T("PT", [128, 2, 3, 512], BF16)
        tmpS = T("tmpS", [128, 2, 512])
        ctmp = tmpS[:, 0, :]
        gates = T("gates", [128, NT, 16])
        lfa = T("lfa", [128, NT, 2, 4])
        cum = T("cum", [128, NT, 16])
        d1 = T("d1", [128, NT, 8])
        e1 = T("e1", [128, NT, 8])
        e2 = T("e2", [128, NT, 8])
        winv = T("winv", [128, NT, 8])
        dec = T("dec", [128, NT, 8])
        Cst = T("Cst", [128, 2, 2, 132])
        qkw = T("qkw", [128, 2, 128], BF16)
        sm = T("sm", [128, 2, 4])
        smh = T("smh", [128, 16])
        smg = T("smg", [128, 2, 3, 4])
        rec = T("rec", [128, 2, 4])

        x1 = R2[:].bitcast(F32).rearrange("p (t d) -> p t d", d=D)
        QT = R2[:, 0:8192].rearrange("p (b t) -> p b t", t=SEQ)
        KT0 = R2[:, 8192:10240]
        KT1 = R2[:, 10240:12288]
        VA = R2[:, 12288:14368].rearrange("p (t c) -> p t c", c=130)
        mQT = R2[:, 14368:16416]
        mKT = R2[:, 16416:18464]
        mV = R2[:, 18464:20544].rearrange("p (t c) -> p t c", c=130)
        Ktok = R2[:, 20544:22592].rearrange("p (t c) -> p t c", c=128)
        KWf = R2[:, 22592:24640].rearrange("p (t c) -> p t c", c=128)
        KWb = R2[:, 24640:26688].rearrange("p (t c) -> p t c", c=128)
        Call = R2[:, 26688:30912].rearrange("p (a t c) -> p a t c", a=2, c=132)
        mixtok = RW[:].rearrange("p (t c) -> p t c", c=1024)
        wup = [RW[:, i * 4096:(i + 1) * 4096].rearrange("p (k f) -> p k f", f=512) for i in (0, 1)]
        wdn = [RW[:, 8192 + i * 4096:8192 + (i + 1) * 4096].rearrange("p (k f) -> p k f", f=1024) for i in (0, 1)]
        wf = [RA[:, i * 1024:(i + 1) * 1024].rearrange("p (k f) -> p k f", f=128) for i in range(3)]
        wt = [RA[:, 3072 + i * 2048:3072 + (i + 1) * 2048].rearrange("p (k f) -> p k f", f=256) for i in range(2)]
        hb = RA[:, 7168:11264].bitcast(F32).rearrange("p (t c) -> p t c", c=128)
        wo = RA[:, 3072:11264].rearrange("p (k f) -> p k f", f=1024)
        hT = [RA[:, i * 2048:(i + 1) * 2048].rearrange("p (k f) -> p k f", f=512) for i in range(2)]
        hf = xs[:].rearrange("p a (t c) -> p (a t) c", c=128)
        ostg = xs[:].rearrange("p a d -> p (a d)")

        banks = [es.enter_context(nc.psum_tensor("bk%d" % i, [128, 512], F32)) for i in range(8)]
        bkey = lambda i: ("bank", i)
        GBK = ["gbc", ("gbcH", 0), ("gbcH", 1)]
        xsk = lambda slot: [("ostg", slot * 8 + i) for i in range(8)]
        ALLX = [("ostg", i) for i in range(16)]

        tri_f = consts[:, 1, :]
        tri_b = consts[:, 2, :]
        ones = consts[:, 3, :]

        P.add("sp", lambda e: e.dma_start(out=consts[:], in_=consts_d), w=["consts"], dma=True)
        bstage = R2[:, 0:6144].bitcast(F32)
        P.add("sp", lambda e: e.dma_start(out=bstage, in_=biasT_d.rearrange("p a b c d -> p (a b c d)")),
              w=["bstage", "R2"], dma=True)
        P.add("dve", lambda e: e.tensor_scalar(out=bstage, in0=bstage, scalar1=8.0, scalar2=None, op0=ALU.mult),
              r=["R2"], w=["bstage"])
        P.add("dve", lambda e: e.tensor_copy(out=bhl[:, 0, :], in_=bstage), r=["bstage", "R2"], w=["bhi"])
        P.add("dve", lambda e: e.tensor_tensor(out=bstage, in0=bstage, in1=bhl[:, 0, :], op=ALU.subtract),
              r=["bhi", "R2"], w=["bstage"])
        P.add("dve", lambda e: e.tensor_copy(out=bhl[:, 1, :], in_=bstage), r=["bstage", "R2"], w=["blo"])
        P.add("sp", lambda e: e.dma_start(out=esink[:], in_=sink_d.partition_broadcast(128)), w=["esink"], dma=True)
        P.add("sp", lambda e: e.dma_start(out=bg[:], in_=bgate_d.partition_broadcast(128)), w=["bg"], dma=True)
        P.add("sp", lambda e: e.dma_start(out=mlg[:], in_=mlg_d.partition_broadcast(128)), w=["mlg"], dma=True)
        P.add("sp", lambda e: e.dma_start(out=convw[:], in_=convw_d), w=["convw"], dma=True)
        P.add("dve", lambda e: e.tensor_copy(out=ident[:], in_=consts[:, 0, :]), r=["consts"], w=["ident"])
        P.add("act", lambda e: e.activation(out=esink[:], in_=esink[:], func=AF.Exp), r=["esink"], w=["esink"])
        P.add("dve", lambda e: e.memset(cst[:], 0.0), w=["cst"])

        def evac_copy(out_ap, in_ap, r, w):
            P.add("act", lambda e: e.activation(out=out_ap, in_=in_ap, func=AF.Copy), r=r, w=w)

        jbank = tmpS[:, 0, :].bitcast(BF16)
        ub2 = [ub[:], PT[:, 0, 0:2, :].rearrange("p a b -> p (a b)")]
        ub2k = [["ub"], [("PT", 0, 0), ("PT", 0, 1)]]

        def norm_square(src_fn, key_fn, j):
            P.add("act", lambda e: e.activation(out=jbank, in_=src_fn(j), func=AF.Square,
                                                accum_out=ssq[:, j:j + 1]),
                  r=key_fn(j), w=[("tmpS", 0), ("ssq", j)])

        def norm_rstd(lo, hi):
            sk = [("ssq", j) for j in range(lo, hi)]
            rk = [("rstd", j) for j in range(lo, hi)]
            P.add("dve", lambda e: e.tensor_scalar(out=rstd[:, lo:hi], in0=ssq[:, lo:hi], scalar1=1.0 / D,
                                                   scalar2=EPS, op0=ALU.mult, op1=ALU.add), r=sk, w=rk)
            P.add("act", lambda e: e.activation(out=rstd[:, lo:hi], in_=rstd[:, lo:hi], func=AF.Sqrt), r=rk, w=rk)
            P.add("dve", lambda e: e.reciprocal(out=rstd[:, lo:hi], in_=rstd[:, lo:hi]), r=rk, w=rk)

        def norm_stats(src_fn, key_fn):
            for hv in range(2):
                for j in range(hv * 8, hv * 8 + 8):
                    norm_square(src_fn, key_fn, j)
                norm_rstd(hv * 8, hv * 8 + 8)

        def norm_apply(src_fn, key_fn, g_key, dst_key, j):
            u_ap = ub2[j % 2]
            u_k = ub2k[j % 2]
            tb = j % 2
            P.add("dve", lambda e: e.scalar_tensor_tensor(
                out=u_ap, in0=src_fn(j), scalar=rstd[:, j:j + 1], in1=gbc[:], op0=ALU.mult, op1=ALU.mult),
                r=list(key_fn(j)) + [("rstd", j)] + GBK, w=u_k)
            psb = banks[tb][:].bitcast(BF16)
            for kc in range(8):
                P.add("pe", lambda e, kc=kc: e.transpose(
                    out=psb[:, kc * 128:(kc + 1) * 128], in_=u_ap[:, kc * 128:(kc + 1) * 128],
                    identity=ident[:]), r=u_k + ["ident"], w=[bkey(tb)])
            evac_copy(R1[:, :, j * 128:(j + 1) * 128], psb.rearrange("p (a b) -> p a b", b=128),
                      r=[bkey(tb)], w=[(dst_key, j), ("R1t", j), "R1"])

        def norm_T(src_fn, key_fn, g_key, dst_key):
            norm_stats(src_fn, key_fn)
            for j in range(NT):
                norm_apply(src_fn, key_fn, g_key, dst_key, j)

        def fence(key):
            P.add("dve", lambda e: e.memset(smh[:, 15:16], 0.0), w=[key, "smh15"], fence=True)

        def final_store(j, s):
            P.add("dve", lambda e: e.scalar_tensor_tensor(
                out=xs[:, j % 2, :], in0=x1[:, j, :], scalar=rstd[:, j:j + 1], in1=gbc[:],
                op0=ALU.mult, op1=ALU.mult), r=[("x1", j), "R2", ("rstd", j)] + GBK, w=xsk(j % 2))
            final_ops.append(P.add("sp", lambda e: e.dma_start(out=out_d[s, j], in_=xs[:, j % 2, :]),
                                   r=xsk(j % 2), dma=True))

        xpref = [0]
        for s in range(nseq):
            fence("R1")
            if s == 0:
                fence("R2")
            P.add("sp", lambda e: e.dma_start(out=gbc[:], in_=g1_d.partition_broadcast(128)), w=GBK, dma=True)
            for j in range(NT):
                if j < xpref[0]:
                    continue
                P.add("sp", lambda e, j=j, s=s: e.dma_start(out=x1[:, j, :], in_=x_d[s, j]),
                      w=[("x1", j), "R2"], dma=True)
            xpref[0] = 0
            norm_T(lambda j: x1[:, j, :], lambda j: [("x1", j), "R2"], "gbc", "uT")

            fence("R2")
            P.add("dve", lambda e: e.memset(R2[:, 8192:12288], 0.0), w=["KT", "R2"])
            P.add("dve", lambda e: e.memset(VA[:, :, 64:65], 1.0), w=["VAones", "R2"])
            P.add("dve", lambda e: e.memset(VA[:, :, 129:130], 1.0), w=["VAones", "R2"])
            P.add("dve", lambda e: e.memset(mV[:, :, 128:129], 1.0), w=["mVones", "R2"])
            fence("RW")
            fence("RA")

            wf_i = [0]

            def proj_fm(blk, dst_fn, keys_w):
                sl = wf_i[0] % 3
                wf_i[0] += 1
                P.add("pool", lambda e: e.dma_start(out=wf[sl], in_=winF_d[blk]), w=[("wf", sl), "RA"], dma=True)
                for tg in range(4):
                    b = tg % 2
                    for kc in range(8):
                        P.add("pe", lambda e, kc=kc, tg=tg, b=b: e.matmul(
                            banks[b][:], lhsT=wf[sl][:, kc, :], rhs=R1[:, kc, tg * 512:(tg + 1) * 512],
                            start=(kc == 0), stop=(kc == 7)),
                            r=[("wf", sl), "RA"] + [("uT", 4 * tg + i) for i in range(4)] + ["R1"], w=[bkey(b)])
                    dst_fn(tg, b)

            if "attn" in stages:
                for blk in range(4):
                    proj_fm(blk, lambda tg, b, blk=blk: evac_copy(
                        QT[:, blk, tg * 512:(tg + 1) * 512], banks[b][:], r=[bkey(b)], w=[("QT", tg), "R2"]), None)

                def kdst(tg, b):
                    P.add("act", lambda e: e.activation(out=KT0[0:64, tg * 512:(tg + 1) * 512],
                                                        in_=banks[b][0:64, :], func=AF.Copy),
                          r=[bkey(b), "KT"], w=[("KTa", tg), "R2"])
                    P.add("dve", lambda e: e.tensor_copy(out=KT1[64:128, tg * 512:(tg + 1) * 512],
                                                         in_=banks[b][64:128, :]),
                          r=[bkey(b), "KT"], w=[("KTb", tg), "R2"])
                proj_fm(4, kdst, None)

            P.add("pool", lambda e: e.dma_start(out=wt[0], in_=winT_d[4]), w=[("wt", 0), "RA"], dma=True)
            for j in range(NT):
                b = 2 + j % 2
                for kc in range(8):
                    P.add("pe", lambda e, kc=kc, j=j, b=b: e.matmul(
                        banks[b][:, 0:144], lhsT=R1[:, kc, j * 128:(j + 1) * 128], rhs=wt[0][:, kc, 0:144],
                        start=(kc == 0), stop=(kc == 7)),
                        r=[("wt", 0), "RA", ("uT", j), "R1"], w=[bkey(b)])
                P.add("dve", lambda e, j=j, b=b: e.tensor_copy(
                    out=VA[:, j, :].rearrange("p (g c) -> p g c", c=65)[:, :, 0:64],
                    in_=banks[b][:, 0:128].rearrange("p (g c) -> p g c", c=64)),
                    r=[bkey(b), "VAones"], w=[("VA", j), "R2"])
                P.add("dve", lambda e, j=j, b=b: e.tensor_tensor(out=gates[:, j, :], in0=banks[b][:, 128:144],
                                                                in1=bg[:], op=ALU.add),
                      r=[bkey(b), "bg"], w=[("gates", j)])

            def attn_gen():
                for j in range(NT):
                    for g in range(2):
                        KTg = KT0 if g == 0 else KT1
                        pslot = g
                        kbs = [kb for kb in range(3) if 0 <= j - 1 + kb < NT]
                        for kb in kbs:
                            jk = j - 1 + kb
                            b = 5 + kb
                            P.add("pe", lambda e, jk=jk, b=b, KTg=KTg, j=j: e.matmul(
                                banks[b][:], lhsT=KTg[:, jk * 128:(jk + 1) * 128],
                                rhs=QT[:, :, j * 128:(j + 1) * 128], start=True, stop=False),
                                r=[("KTa", jk // 4), ("KTb", jk // 4), "KT", ("QT", j // 4), "R2"], w=[bkey(b)])
                            boff = (kb * 2 + g) * 512
                            for hl in range(2):
                                P.add("pe", lambda e, b=b, hl=hl, boff=boff: e.matmul(
                                    banks[b][:], lhsT=ident[:], rhs=bhl[:, hl, boff:boff + 512],
                                    start=False, stop=(hl == 1)),
                                    r=["ident", "bhi", "blo"], w=[bkey(b)])
                            P.add("act", lambda e, kb=kb, b=b, pslot=pslot: e.activation(
                                out=PT[:, pslot, kb, :], in_=banks[b][:], func=AF.Exp, scale=0.125),
                                r=[bkey(b)], w=[("PT", pslot, kb)])
                        ob = 4
                        for jh in range(4):
                            for i, kb in enumerate(kbs):
                                jk = j - 1 + kb
                                last = (i == len(kbs) - 1)
                                P.add("pe", lambda e, jh=jh, kb=kb, jk=jk, i=i, ob=ob, g=g, pslot=pslot, last=last: e.matmul(
                                    banks[ob][:, jh * 65:(jh + 1) * 65],
                                    lhsT=PT[:, pslot, kb, jh * 128:(jh + 1) * 128],
                                    rhs=VA[:, jk, g * 65:(g + 1) * 65],
                                    start=(i == 0), stop=last),
                                    r=[("PT", pslot, kb), ("VA", jk), "VAones", "R2"], w=[bkey(ob)])
                        O3 = banks[ob][:, 0:260].rearrange("p (h c) -> p h c", c=65)
                        P.add("dve", lambda e, O3=O3, g=g: e.tensor_tensor(
                            out=sm[:, g, :], in0=O3[:, :, 64], in1=esink[:, g * 4:(g + 1) * 4], op=ALU.add),
                            r=[bkey(ob), "esink"], w=[("sm", g)])
                        P.add("dve", lambda e, g=g: e.reciprocal(out=rec[:, g, :], in_=sm[:, g, :]),
                              r=[("sm", g)], w=[("rec", g)])
                        P.add("dve", lambda e, O3=O3, g=g, j=j: e.tensor_tensor(
                            out=mixtok[:, j, g * 256:(g + 1) * 256].rearrange("p (h c) -> p h c", c=64),
                            in0=O3[:, :, 0:64], in1=rec[:, g, :].unsqueeze(2).to_broadcast([128, 4, 64]),
                            op=ALU.mult),
                            r=[bkey(ob), ("rec", g)], w=[("mixtok", j, g), "RW"])
                    yield

            def ml_gen():
                g4 = gates[:].rearrange("p t (a b) -> p t a b", b=4)
                gk = [("gates", j) for j in range(NT)]
                P.add("act", lambda e: e.activation(out=lfa[:], in_=g4[:, :, 1::2, :], func=AF.Exp, scale=-1.0),
                      r=gk, w=["lfa"])
                P.add("dve", lambda e: e.tensor_scalar(out=lfa[:], in0=lfa[:], scalar1=1.0, scalar2=None,
                                                       op0=ALU.add), r=["lfa"], w=["lfa"])
                P.add("act", lambda e: e.activation(out=lfa[:], in_=lfa[:], func=AF.Ln), r=["lfa"], w=["lfa"])
                P.add("dve", lambda e: e.tensor_scalar(out=lfa[:], in0=lfa[:], scalar1=-1.0, scalar2=None,
                                                       op0=ALU.mult), r=["lfa"], w=["lfa"])
                cb = 3
                for j in range(NT):
                    for di, tri in enumerate((tri_f, tri_b)):
                        P.add("pe", lambda e, j=j, di=di, tri=tri: e.matmul(
                            banks[cb][:, j * 16 + di * 4:j * 16 + di * 4 + 4], lhsT=tri, rhs=lfa[:, j, di, :],
                            start=True, stop=True), r=["lfa", "consts"], w=[bkey(cb)])
                        P.add("pe", lambda e, j=j, di=di: e.matmul(
                            banks[cb][:, j * 16 + 8 + di * 4:j * 16 + 8 + di * 4 + 4], lhsT=ones, rhs=lfa[:, j, di, :],
                            start=True, stop=True), r=["lfa", "consts"], w=[bkey(cb)])
                P.add("dve", lambda e: e.tensor_copy(out=cum[:].rearrange("p t c -> p (t c)"), in_=banks[cb][:, 0:256]),
                      r=[bkey(cb)], w=["cum"])
                bcum = cum[:, :, 0:8].rearrange("p t (a b) -> p t a b", b=4)
                tot = cum[:, :, 8:16]
                d1v = d1[:].rearrange("p t (a b) -> p t a b", b=4)
                P.add("dve", lambda e: e.scalar_tensor_tensor(out=d1v, in0=g4[:, :, 0::2, :], scalar=LN_KSCALE,
                                                              in1=bcum, op0=ALU.add, op1=ALU.subtract),
                      r=gk + ["cum"], w=["d1"])
                P.add("act", lambda e: e.activation(out=e1[:], in_=d1[:], func=AF.Exp), r=["d1"], w=["e1"])
                P.add("act", lambda e: e.activation(out=winv[:], in_=cum[:, :, 0:8], func=AF.Exp, scale=-1.0),
                      r=["cum"], w=["winv"])
                P.add("dve", lambda e: e.tensor_tensor(out=d1[:], in0=d1[:], in1=tot, op=ALU.add),
                      r=["d1", "cum", "e1"], w=["d1"])
                P.add("act", lambda e: e.activation(out=e2[:], in_=d1[:], func=AF.Exp), r=["d1"], w=["e2"])
                P.add("act", lambda e: e.activation(out=dec[:], in_=tot, func=AF.Exp), r=["cum"], w=["dec"])
                yield

                for h in range(4):
                    wsl = 1
                    P.add("pool", lambda e, h=h: e.dma_start(out=wt[wsl], in_=winT_d[h]),
                          w=[("wt", wsl), "RA"], dma=True)

                    def vo_tiles(j0, j1):
                        for j in range(j0, j1, 2):
                            b = 2 + (j // 2) % 2
                            for jj in range(2):
                                for kc in range(8):
                                    P.add("pe", lambda e, kc=kc, j=j, jj=jj, b=b: e.matmul(
                                        banks[b][:, jj * 256:(jj + 1) * 256],
                                        lhsT=R1[:, kc, (j + jj) * 128:(j + jj + 1) * 128], rhs=wt[wsl][:, kc, :],
                                        start=(kc == 0), stop=(kc == 7)),
                                        r=[("wt", wsl), "RA", ("uT", j + jj), "R1"], w=[bkey(b)])
                            pv = banks[b][:].rearrange("p (t c) -> p t c", c=256)
                            P.add("act", lambda e, j=j, pv=pv: e.activation(out=mV[:, j:j + 2, 0:128], in_=pv[:, :, 0:128],
                                                                          func=AF.Copy),
                                  r=[bkey(b), "mVones"], w=[("mV", j), ("mV", j + 1), "R2"])
                            P.add("act", lambda e, j=j, pv=pv, h=h: e.activation(
                                out=mixtok[:, j:j + 2, 512 + h * 128:512 + (h + 1) * 128], in_=pv[:, :, 128:256],
                                func=AF.Sigmoid), r=[bkey(b)], w=[("sigo", h, j), ("sigo", h, j + 1), "RW"])

                    for qk, blk, dstT, dkey in ((0, 5 + h, mQT, "mQT"), (1, 9 + h, mKT, "mKT")):
                        cblk = qk * 4 + h

                        def cdst(tg, b):
                            P.add("act", lambda e: e.activation(out=cst[:, 1 + tg * 512:1 + (tg + 1) * 512],
                                                                in_=banks[b][:], func=AF.Copy),
                                  r=[bkey(b)], w=[("cst", tg)])
                        proj_fm(blk, cdst, None)
                        vo_tiles(qk * 8, qk * 8 + 8)
                        for tg in range(4):
                            rk = [("cst", t) for t in range(max(0, tg - 1), min(4, tg + 2))] + ["cst", "convw"]
                            lo = tg * 512
                            ct = (tmpS[:, 0, :], tmpS[:, 1, :], gbc[:, 0:512], gbc[:, 512:1024])[tg]
                            ck = (("tmpS", 0), ("tmpS", 1), ("gbcH", 0), ("gbcH", 1))[tg]
                            P.add("dve", lambda e, lo=lo, cblk=cblk, ct=ct: e.tensor_scalar(
                                out=ct, in0=cst[:, lo:lo + 512], scalar1=convw[:, cblk, 0:1], scalar2=None,
                                op0=ALU.mult), r=rk, w=[ck])
                            P.add("dve", lambda e, lo=lo, cblk=cblk, ct=ct: e.scalar_tensor_tensor(
                                out=ct, in0=cst[:, lo + 1:lo + 513], scalar=convw[:, cblk, 1:2], in1=ct,
                                op0=ALU.mult, op1=ALU.add), r=rk + [ck], w=[ck])
                            P.add("dve", lambda e, lo=lo, cblk=cblk, ct=ct: e.scalar_tensor_tensor(
                                out=ct, in0=cst[:, lo + 2:lo + 514], scalar=convw[:, cblk, 2:3], in1=ct,
                                op0=ALU.mult, op1=ALU.add), r=rk + [ck], w=[ck])
                            P.add("act", lambda e, lo=lo, dstT=dstT, ct=ct: e.activation(
                                out=dstT[:, lo:lo + 512], in_=ct, func=AF.Silu),
                                r=[ck], w=[(dkey, tg), "R2"])
                        yield
                    for half in range(2):
                        tb = half
                        psb = banks[tb][:].bitcast(BF16)
                        for c8 in range(8):
                            c = half * 8 + c8
                            P.add("pe", lambda e, c=c, c8=c8, psb=psb: e.transpose(
                                out=psb[:, c8 * 128:(c8 + 1) * 128], in_=mKT[:, c * 128:(c + 1) * 128],
                                identity=ident[:]),
                                r=[("mKT", c // 4), "ident", "R2"], w=[bkey(tb)])
                        evac_copy(Ktok[:, half * 8:(half + 1) * 8, :], psb.rearrange("p (a b) -> p a b", b=128),
                                  r=[bkey(tb)], w=[("Ktok", half), "R2"])
                    for di, KW in ((0, KWf), (1, KWb)):
                        P.add("dve", lambda e, di=di, KW=KW, h=h: e.tensor_tensor(
                            out=KW, in0=Ktok, in1=e2[:, :, di * 4 + h:di * 4 + h + 1].to_broadcast([128, NT, 128]),
                            op=ALU.mult), r=[("Ktok", 0), ("Ktok", 1), "e2", "R2"], w=[("KW", di), "R2"])
                    P.add("dve", lambda e: e.memset(Cst[:], 0.0), w=[("C", 0, 0), ("C", 0, 1), ("C", 1, 0), ("C", 1, 1)])
                    P.add("dve", lambda e: e.memset(Call[:, 0, 0, :], 0.0), w=[("Cb", 0, 0), "R2"])
                    P.add("dve", lambda e: e.memset(Call[:, 1, NT - 1, :], 0.0), w=[("Cb", 1, NT - 1), "R2"])
                    for grp in range(5):
                        for di in range(2):
                            col = di * 4 + h
                            KW = KWf if di == 0 else KWb
                            bnk = (2 * grp + di) % 4
                            for i in range(3):
                                step = 3 * grp + i
                                c = step if di == 0 else NT - 1 - step
                                P.add("pe", lambda e, c=c, bnk=bnk, KW=KW, i=i: e.matmul(
                                    banks[bnk][:, i * 129:(i + 1) * 129], lhsT=KW[:, c, :], rhs=mV[:, c, 0:129],
                                    start=True, stop=True),
                                    r=[("KW", di), ("mV", c), "mVones", "R2"], w=[bkey(bnk)])
                            for i in range(3):
                                step = 3 * grp + i
                                c = step if di == 0 else NT - 1 - step
                                cn = c + 1 if di == 0 else c - 1
                                sbuf_, dbuf_ = step % 2, (step + 1) % 2
                                P.add("dve", lambda e, c=c, bnk=bnk, col=col, di=di, i=i, sbuf_=sbuf_, dbuf_=dbuf_:
                                      e.scalar_tensor_tensor(
                                          out=Cst[:, di, dbuf_, 0:129], in0=Cst[:, di, sbuf_, 0:129],
                                          scalar=dec[:, c, col:col + 1],
                                          in1=banks[bnk][:, i * 129:(i + 1) * 129], op0=ALU.mult, op1=ALU.add),
                                      r=[bkey(bnk), ("C", di, sbuf_), "dec"], w=[("C", di, dbuf_)])
                                P.add("act", lambda e, di=di, cn=cn, dbuf_=dbuf_: e.activation(
                                    out=Call[:, di, cn, 0:129], in_=Cst[:, di, dbuf_, 0:129], func=AF.Copy),
                                    r=[("C", di, dbuf_)], w=[("Cb", di, cn), "R2"])
                        if grp == 2:
                            yield
                    yield
                    groups = [(0, 3), (3, 3), (6, 3), (9, 3), (12, 3), (15, 1)]
                    pendB = []

                    def stageB(gi, di, c0, n, nbk):
                        col = di * 4 + h
                        hdst = hf if di == 0 else hb
                        N3 = banks[nbk][:, 0:n * 129].rearrange("p (i c) -> p i c", c=129)
                        P.add("dve", lambda e: e.tensor_tensor(
                            out=smg[:, di, 1, 0:n], in0=smg[:, di, 0, 0:n], in1=winv[:, c0:c0 + n, col],
                            op=ALU.max), r=[("smg", di, 0), "winv"], w=[("smg", di, 1)])
                        P.add("dve", lambda e: e.reciprocal(out=smg[:, di, 2, 0:n], in_=smg[:, di, 1, 0:n]),
                              r=[("smg", di, 1)], w=[("smg", di, 2)])
                        P.add("dve", lambda e: e.tensor_tensor(
                            out=hdst[:, c0:c0 + n, :], in0=N3[:, :, 0:128],
                            in1=smg[:, di, 2, 0:n].unsqueeze(2).to_broadcast([128, n, 128]), op=ALU.mult),
                            r=[bkey(nbk), ("smg", di, 2)],
                            w=[("h", di, c0 + i) for i in range(n)]
                            + ([("ostg", c0 + i) for i in range(n)] if di == 0 else ["RA"]))

                    units = [(gi, di, c0, n) for gi, (c0, n) in enumerate(groups) for di in range(2)]

                    def emit_st(u):
                        gi, di, c0, n = units[u]
                        sbk = u % 2
                        for i in range(n):
                            c = c0 + i
                            P.add("pe", lambda e, c=c, sbk=sbk, i=i: e.matmul(
                                banks[sbk][:, i * 128:(i + 1) * 128], lhsT=mKT[:, c * 128:(c + 1) * 128],
                                rhs=mQT[:, c * 128:(c + 1) * 128], start=True, stop=True),
                                r=[("mKT", c // 4), ("mQT", c // 4), "R2"], w=[bkey(sbk)])

                    emit_st(0)
                    for u, (gi, di, c0, n) in enumerate(units):
                        col = di * 4 + h
                        mask = tri_f if di == 0 else tri_b
                        QK = KWf if di == 0 else KWb
                        sbk = u % 2
                        nbk = 2 + u % 2
                        if u + 1 < len(units):
                            emit_st(u + 1)
                        for i in range(n):
                            c = c0 + i
                            first = (gi == 0 and i == 0)
                            P.add("dve", lambda e, c=c, sbk=sbk, col=col, mask=mask, QK=QK, i=i: e.scalar_tensor_tensor(
                                out=QK[:, c, :], in0=banks[sbk][:, i * 128:(i + 1) * 128],
                                scalar=e1[:, c, col:col + 1], in1=mask, op0=ALU.mult, op1=ALU.mult),
                                r=[bkey(sbk), "e1", "consts"] + ([] if first else [("KW", di)]),
                                w=[("qk", di, c), "R2"] + ([("KW", di)] if first else []))
                        if len(pendB) > 1:
                            stageB(*pendB.pop(0))
                        for i in range(n):
                            c = c0 + i
                            P.add("pe", lambda e, c=c, nbk=nbk, QK=QK, i=i: e.matmul(
                                banks[nbk][:, i * 129:(i + 1) * 129], lhsT=QK[:, c, :], rhs=mV[:, c, 0:129],
                                start=True, stop=False),
                                r=[("qk", di, c), ("KW", di), ("mV", c), "mVones", "R2"], w=[bkey(nbk)])
                            P.add("pe", lambda e, c=c, nbk=nbk, di=di, i=i: e.matmul(
                                banks[nbk][:, i * 129:(i + 1) * 129], lhsT=mQT[:, c * 128:(c + 1) * 128],
                                rhs=Call[:, di, c, 0:129], start=False, stop=True),
                                r=[("mQT", c // 4), ("Cb", di, c), "R2"], w=[bkey(nbk)])
                        N3 = banks[nbk][:, 0:n * 129].rearrange("p (i c) -> p i c", c=129)
                        P.add("act", lambda e, N3=N3, di=di, n=n: e.activation(
                            out=smg[:, di, 0, 0:n], in_=N3[:, :, 128], func=AF.Abs),
                            r=[bkey(nbk)], w=[("smg", di, 0)])
                        pendB.append((gi, di, c0, n, nbk))
                        if u % 4 == 3:
                            yield
                    while pendB:
                        stageB(*pendB.pop(0))
                    hkeys = [("h", di, c) for di in range(2) for c in range(NT)]
                    P.add("dve", lambda e: e.tensor_tensor(out=hf, in0=hf, in1=hb, op=ALU.add),
                          r=hkeys + ["RA"], w=["hsum"] + ALLX)
                    for c in range(NT):
                        P.add("act", lambda e, c=c: e.activation(out=qkw[:, 0, :], in_=hf[:, c, :], func=AF.Square,
                                                                 accum_out=ssq[:, c:c + 1]),
                              r=["hsum"], w=[("qkw", 0), ("ssq", c)])
                    sk = [("ssq", c) for c in range(NT)]
                    P.add("dve", lambda e: e.tensor_scalar(out=ssq[:], in0=ssq[:], scalar1=1.0 / 128, scalar2=EPS,
                                                           op0=ALU.mult, op1=ALU.add), r=sk, w=sk)
                    P.add("act", lambda e: e.activation(out=ssq[:], in_=ssq[:], func=AF.Sqrt), r=sk, w=sk)
                    P.add("dve", lambda e: e.reciprocal(out=ssq[:], in_=ssq[:]), r=sk, w=sk)
                    P.add("dve", lambda e: e.tensor_tensor(out=hf, in0=hf,
                                                           in1=ssq[:].unsqueeze(2).to_broadcast([128, NT, 128]),
                                                           op=ALU.mult), r=["hsum"] + sk, w=["hsum"])
                    P.add("dve", lambda e, h=h: e.tensor_tensor(
                        out=hf, in0=hf,
                        in1=mlg[:, h * 128:(h + 1) * 128].unsqueeze(1).to_broadcast([128, NT, 128]),
                        op=ALU.mult), r=["hsum", "mlg"], w=["hsum"])
                    P.add("dve", lambda e, h=h: e.tensor_tensor(
                        out=mixtok[:, :, 512 + h * 128:512 + (h + 1) * 128], in0=hf,
                        in1=mixtok[:, :, 512 + h * 128:512 + (h + 1) * 128], op=ALU.mult),
                        r=["hsum"] + [("sigo", h, j) for j in range(NT)] + ALLX, w=[("mlout", h), "RW"])
                    yield "F"

            gm = ml_gen() if "ml" in stages else None
            ga = attn_gen() if "attn" in stages else None

            def adv(g):
                try:
                    return next(g), g
                except StopIteration:
                    return None, None

            tick = 0
            nF = [0]
            while gm is not None:
                tag, gm = adv(gm)
                tick += 1
                if ga is not None:
                    if tag == "F":
                        n_att = (5, 5, 6, 0)[nF[0] % 4]
                        nF[0] += 1
                    else:
                        n_att = 0
                    for _ in range(n_att):
                        if ga is not None:
                            _, ga = adv(ga)
            while ga is not None:
                _, ga = adv(ga)

            x1src = lambda j: x1[:, j, :]
            x1key = lambda j: [("x1", j), "R2"]
            fuse_norm2 = ("oproj" in stages) and ("ffn" in stages)
            if "oproj" in stages:
                fence("R2")
                fence("RA")
                for j in range(NT):
                    P.add("sp", lambda e, j=j, s=s: e.dma_start(out=x1[:, j, :], in_=x_d[s, j]),
                          w=[("x1", j), "R2"], dma=True)
                P.add("pool", lambda e: e.dma_start(out=wo, in_=wout_d), w=["wo", "RA"], dma=True)
                fence("R1")
                for j in range(NT):
                    tb = j % 2
                    psb = banks[tb][:].bitcast(BF16)
                    rk = [("mixtok", j, 0), ("mixtok", j, 1)] + [("mlout", h) for h in range(4)] + ["RW"]
                    for c in range(8):
                        P.add("pe", lambda e, c=c, j=j, psb=psb: e.transpose(
                            out=psb[:, c * 128:(c + 1) * 128], in_=mixtok[:, j, c * 128:(c + 1) * 128],
                            identity=ident[:]), r=rk + ["ident"], w=[bkey(tb)])
                    evac_copy(R1[:, :, j * 128:(j + 1) * 128], psb.rearrange("p (a b) -> p a b", b=128),
                              r=[bkey(tb)], w=[("mixT", j), ("R1t", j), "R1"])
                if fuse_norm2:
                    fence("RW")
                    P.add("pool", lambda e: e.dma_start(out=wup[0], in_=wup_d[0]), w=[("wup", 0), "RW"], dma=True)
                    P.add("pool", lambda e: e.dma_start(out=wdn[0], in_=wdn_d[0]), w=[("wdn", 0), "RW"], dma=True)
                    P.add("sp", lambda e: e.dma_start(out=gbc[:], in_=g2_d.partition_broadcast(128)),
                          w=GBK, dma=True)
                for j in range(NT):
                    for dh in range(2):
                        b = 2 + (2 * j + dh) % 4
                        for c in range(8):
                            P.add("pe", lambda e, c=c, j=j, dh=dh, b=b: e.matmul(
                                banks[b][:], lhsT=R1[:, c, j * 128:(j + 1) * 128],
                                rhs=wo[:, c, dh * 512:(dh + 1) * 512], start=(c == 0), stop=(c == 7)),
                                r=[("mixT", j), ("R1t", j), "R1", "wo", "RA"], w=[bkey(b)])
                        P.add("dve", lambda e, j=j, dh=dh, b=b: e.tensor_tensor(
                            out=x1[:, j, dh * 512:(dh + 1) * 512], in0=banks[b][:],
                            in1=x1[:, j, dh * 512:(dh + 1) * 512], op=ALU.add),
                            r=[bkey(b), ("x1", j)], w=[("x1", j), "R2"])
                    if fuse_norm2:
                        norm_square(x1src, x1key, j)
                        if j % 8 == 7:
                            norm_rstd(j - 7, j + 1)
                        if j >= 8:
                            norm_apply(x1src, x1key, "gbc", "xnT", j - 8)
                if fuse_norm2:
                    for j in range(8, NT):
                        norm_apply(x1src, x1key, "gbc", "xnT", j)
            else:
                fence("R2")
                for j in range(NT):
                    P.add("sp", lambda e, j=j, s=s: e.dma_start(out=x1[:, j, :], in_=x_d[s, j]),
                          w=[("x1", j), "R2"], dma=True)

            if "ffn" in stages:
                if not fuse_norm2:
                    fence("R1")
                    fence("RW")
                    P.add("sp", lambda e: e.dma_start(out=gbc[:], in_=g2_d.partition_broadcast(128)),
                          w=GBK, dma=True)
                    norm_T(x1src, x1key, "gbc", "xnT")
                fence("RA")
                def ffn_up(fb, tg):
                    sl = fb % 2
                    hs = (fb * 4 + tg) % 2
                    if tg == 0:
                        if not (fuse_norm2 and fb == 0):
                            P.add("pool", lambda e: e.dma_start(out=wup[sl], in_=wup_d[fb]),
                                  w=[("wup", sl), "RW"], dma=True)
                            P.add("pool", lambda e: e.dma_start(out=wdn[sl], in_=wdn_d[fb]),
                                  w=[("wdn", sl), "RW"], dma=True)
                        if fb == 7:
                            P.add("sp", lambda e: e.dma_start(out=gbc[:], in_=gf_d.partition_broadcast(128)),
                                  w=GBK, dma=True)
                    for fc in range(4):
                        b = 4 + fc
                        for kc in range(8):
                            P.add("pe", lambda e, kc=kc, fc=fc, b=b: e.matmul(
                                banks[b][:], lhsT=wup[sl][:, kc, fc * 128:(fc + 1) * 128],
                                rhs=R1[:, kc, tg * 512:(tg + 1) * 512], start=(kc == 0), stop=(kc == 7)),
                                r=[("wup", sl), "RW", "R1"] + [("xnT", 4 * tg + i) for i in range(4)],
                                w=[bkey(b)])
                        ts = fc % 2
                        P.add("act", lambda e, b=b, ts=ts: e.activation(out=tmpS[:, ts, :], in_=banks[b][:],
                                                                        func=AF.Relu),
                              r=[bkey(b)], w=[("tmpS", ts)])
                        P.add("act", lambda e, fc=fc, ts=ts: e.activation(
                            out=hT[hs][:, fc, :], in_=tmpS[:, ts, :], func=AF.Square),
                            r=[("tmpS", ts)], w=[("hT", hs, fc), "RA"])

                def ffn_down(fb, tg, s):
                    sl = fb % 2
                    hs = (fb * 4 + tg) % 2
                    if fb == 7 and tg == 3 and s + 1 < nseq:
                        for j in range(12):
                            P.add("sp", lambda e, j=j: e.dma_start(out=x1[:, j, :], in_=x_d[s + 1, j]),
                                  w=[("x1", j), "R2"], dma=True)
                        xpref[0] = 12
                    for t in range(4):
                        j = tg * 4 + t
                        for dh in range(2):
                            b = 2 * (t % 2) + dh
                            for fc in range(4):
                                P.add("pe", lambda e, fc=fc, t=t, dh=dh, b=b: e.matmul(
                                    banks[b][:], lhsT=hT[hs][:, fc, t * 128:(t + 1) * 128],
                                    rhs=wdn[sl][:, fc, dh * 512:(dh + 1) * 512],
                                    start=(fc == 0), stop=(fc == 3)),
                                    r=[("hT", hs, fc), "RA", ("wdn", sl), "RW"], w=[bkey(b)])
                            P.add("dve", lambda e, j=j, dh=dh, b=b: e.tensor_tensor(
                                out=x1[:, j, dh * 512:(dh + 1) * 512], in0=banks[b][:],
                                in1=x1[:, j, dh * 512:(dh + 1) * 512], op=ALU.add),
                                r=[bkey(b), ("x1", j)], w=[("x1", j), "R2"])
                        if fb == 7:
                            norm_square(x1src, x1key, j)
                    if fb == 7:
                        norm_rstd(tg * 4, tg * 4 + 4)
                        for j in range(tg * 4, tg * 4 + 4):
                            final_store(j, s)

                steps = [(fb, tg) for fb in range(8) for tg in range(4)]
                ffn_up(*steps[0])
                for i, (fb, tg) in enumerate(steps):
                    if i + 1 < len(steps):
                        ffn_up(*steps[i + 1])
                    ffn_down(fb, tg, s)
            if "ffn" not in stages:
                P.add("sp", lambda e: e.dma_start(out=gbc[:], in_=gf_d.partition_broadcast(128)), w=GBK, dma=True)
                norm_stats(x1src, x1key)
                for j in range(NT):
                    final_store(j, s)

        for name, getter, shape, dt in taps:
            t_d = dram(name, shape, kind="ExternalOutput", dt=dt)
            loc = dict(locals())
            ap = getter(loc)
            keys = list(P.last_w.keys())
            final_ops.append(P.add("sp", lambda e, t_d=t_d, ap=ap: e.dma_start(out=t_d, in_=ap), r=keys, dma=True))
        P.emit(nc, final_wait_ops=final_ops)
    return nc


_NC_CACHE = {}


def kernel(**inputs):
    shared = _host_layout(inputs)
    x = np.asarray(inputs["x"], np.float32)
    xs = x.reshape(NCORES, NSEQ, NT, 128, D)
    if "nc" not in _NC_CACHE:
        _NC_CACHE["nc"] = build_nc()
    nc = _NC_CACHE["nc"]
    in_maps = []
    for c in range(NCORES):
        m = dict(shared)
        m["x"] = np.ascontiguousarray(xs[c])
        in_maps.append(m)
    res = run_bass_kernel_spmd(nc, in_maps, core_ids=list(range(NCORES)))
    out = np.stack([np.asarray(r["out"], np.float32) for r in res.results])
    return out.reshape(16, SEQ, D)
```
